# Optimizing a Trainium2 kernel written in Bass

```python
import math
import jax, jax.numpy as jnp
from jax import lax
import numpy as np

D_MODEL = 1024
BATCH = 8
SEQ = 2048
DEPTH = 2
DEC_BATCH = 128
DEC_SEQ = 8
PAST_LEN = 16384
PAGE_SIZE = 128

D_MIX = D_MODEL
RWKV_DIM = D_MIX // 2
RWKV_HEAD = 64
RWKV_HEADS = RWKV_DIM // RWKV_HEAD
LRU_DIM = D_MIX // 4
LRU_BLOCKS = 4
LRU_BLOCK = LRU_DIM // LRU_BLOCKS
CONV_W = 4
LRU_C = 8.0
S5_DIM = D_MIX - RWKV_DIM - LRU_DIM
S5_GROUP = 16
S5_GROUPS = S5_DIM // S5_GROUP
S5_STATE = 64
D_IN = 3 * RWKV_DIM + 2 * LRU_DIM + S5_DIM
LORA_DECAY = 64
LORA_A = 64
LORA_MV = 32
LORA_GATE = 160
N_MEM = 256
MEM_HEADS = 4
MEM_HEAD_DIM = D_MODEL // MEM_HEADS
D_FF = -(-8 * D_MODEL // (3 * 256)) * 256
RMS_EPS = 1e-6
GN_EPS = 64e-5

kernel_name = 'hybrid_rwkv7_rglru_s5_decoder'


def rms_norm(x, g):
    xf = x.astype(jnp.float32)
    y = xf * lax.rsqrt(jnp.mean(xf * xf, axis=-1, keepdims=True) + RMS_EPS)
    return (y * g.astype(jnp.float32)).astype(x.dtype)


def _shift(seq, first):
    return jnp.concatenate([first[:, None].astype(seq.dtype), seq[:, :-1]], axis=1)


def _split_heads(t):
    return t.reshape(t.shape[:-1] + (RWKV_HEADS, RWKV_HEAD))


def _linear_combine(c1, c2):
    a1, b1 = c1
    a2, b2 = c2
    return a1 * a2, a2 * b1 + b2


def rwkv7_time_mix(h, h_prev, z_rkv, w_rkv, wkv0, v_first, vres, mu_rkv, mu_wag,
                   w0, w1, w2, a0, a1, a2, g1, g2, k_k, k_a, r_k, ln_w, ln_b):
    f32 = jnp.float32
    B, L, _ = h.shape
    dh = _shift(h, h_prev) - h
    z_sh = _shift(z_rkv, h_prev @ w_rkv)
    z_mix = z_rkv + (z_sh - z_rkv) * mu_rkv
    r, k, v = jnp.split(z_mix, 3, axis=-1)
    xw = h + dh * mu_wag[0]
    xa = h + dh * mu_wag[1]
    xg = h + dh * mu_wag[2]
    w_log = -jax.nn.softplus(-(w0 + jnp.tanh(xw @ w1) @ w2)) - 0.5
    a = jax.nn.sigmoid(a0 + (xa @ a1) @ a2)
    g = jax.nn.sigmoid(xg @ g1) @ g2
    if vres is None:
        v_first = v
    else:
        mu_v, v0, v1, v2 = vres
        xv = h + dh * mu_v
        v = v + (v_first - v) * jax.nn.sigmoid(v0 + (xv @ v1) @ v2)
    kk = _split_heads((k * k_k).astype(f32))
    kk = kk / jnp.maximum(jnp.linalg.norm(kk, axis=-1, keepdims=True), 1e-12)
    k = k * (1.0 + (a - 1.0) * k_a)
    rh, kh, vh, ah = [_split_heads(t.astype(f32)) for t in (r, k, v, a)]
    decay = jnp.exp(-jnp.exp(_split_heads(w_log.astype(f32))))
    seq = tuple(jnp.moveaxis(t, 1, 0) for t in (rh, decay, kh, vh, -kk, kk * ah))

    def step(S, inp):
        r_t, w_t, k_t, v_t, a_t, b_t = inp
        sa = jnp.einsum('bhij,bhj->bhi', S, a_t)
        S = (S * w_t[:, :, None, :] + sa[..., None] * b_t[:, :, None, :]
             + v_t[..., None] * k_t[:, :, None, :])
        return S, jnp.einsum('bhij,bhj->bhi', S, r_t)

    S_last, ys = lax.scan(step, wkv0.astype(f32), seq)
    y = jnp.moveaxis(ys, 0, 1)
    mu = jnp.mean(y, axis=-1, keepdims=True)
    var = jnp.mean(jnp.square(y - mu), axis=-1, keepdims=True)
    y = ((y - mu) * lax.rsqrt(var + GN_EPS)).reshape(B, L, RWKV_DIM)
    y = y * ln_w.astype(f32) + ln_b.astype(f32)
    bonus = jnp.sum(rh * kh * r_k.astype(f32), axis=-1, keepdims=True) * vh
    y = (y + bonus.reshape(B, L, RWKV_DIM)) * g.astype(f32)
    return y.astype(h.dtype), S_last, v_first


def rg_lru_branch(zx, zg, conv0, h0, conv_w, conv_b, wa, ba, wi, bi, lam, reset_first):
    f32 = jnp.float32
    B, L, C = zx.shape
    ext = jnp.concatenate([conv0.astype(zx.dtype), zx], axis=1)
    xc = conv_b + sum(ext[:, j:j + L] * conv_w[j] for j in range(CONV_W))
    new_conv = ext[:, L:]
    xb = xc.reshape(B, L, LRU_BLOCKS, LRU_BLOCK)
    gate_a = jax.nn.sigmoid((jnp.einsum('blhi,hij->blhj', xb, wa).reshape(B, L, C) + ba).astype(f32))
    gate_i = jax.nn.sigmoid((jnp.einsum('blhi,hij->blhj', xb, wi).reshape(B, L, C) + bi).astype(f32))
    log_a = -LRU_C * gate_a * jax.nn.softplus(-lam.astype(f32))
    a = jnp.exp(log_a)
    mult = jnp.sqrt(-jnp.expm1(2.0 * log_a))
    if reset_first:
        mult = mult.at[:, 0].set(1.0)
    b = mult * gate_i * xc.astype(f32)
    b = b.at[:, 0].add(a[:, 0] * h0.astype(f32))
    _, hs = lax.associative_scan(_linear_combine, (a, b), axis=1)
    y = hs * jax.nn.gelu(zg.astype(f32))
    return y.astype(zx.dtype), new_conv, hs[:, -1]


def s5_branch(u, s_re0, s_im0, a_re, a_im, log_dt, b_re, b_im, c_re, c_im, d, w_glu, b_glu):
    f32 = jnp.float32
    B, L, _ = u.shape
    lam = lax.complex(a_re.astype(f32), a_im.astype(f32))
    dt = jnp.exp(log_dt.astype(f32))[:, None]
    a_bar = jnp.exp(lam * dt)
    b_bar = ((a_bar - 1.0) / lam)[..., None] * lax.complex(b_re.astype(f32), b_im.astype(f32))
    uf = u.astype(f32)
    ug = uf.reshape(B, L, S5_GROUPS, S5_GROUP).astype(jnp.complex64)
    bu = jnp.einsum('blgp,gnp->blgn', ug, b_bar)
    bu = bu.at[:, 0].add(a_bar * lax.complex(s_re0.astype(f32), s_im0.astype(f32)))
    _, xs = lax.associative_scan(_linear_combine, (jnp.broadcast_to(a_bar, bu.shape), bu), axis=1)
    c = lax.complex(c_re.astype(f32), c_im.astype(f32))
    y = jnp.real(jnp.einsum('blgn,gpn->blgp', xs, c)).reshape(B, L, S5_DIM) + d.astype(f32) * uf
    y = jax.nn.gelu(y).astype(u.dtype)
    y1, y2 = jnp.split(y @ w_glu + b_glu, 2, axis=-1)
    out = y1 * jax.nn.sigmoid(y2)
    return out, jnp.real(xs[:, -1]), jnp.imag(xs[:, -1])


def mem_project(mem, g, wk, wv):
    B, M, _ = mem.shape
    m = rms_norm(mem, g)
    k = (m @ wk).reshape(B, M, MEM_HEADS, MEM_HEAD_DIM)
    v = (m @ wv).reshape(B, M, MEM_HEADS, MEM_HEAD_DIM)
    return k, v


def mem_attend(hq, mk, mv, wq, wo):
    B, L, _ = hq.shape
    q = (hq @ wq).reshape(B, L, MEM_HEADS, MEM_HEAD_DIM)
    s = jnp.einsum('blhd,bmhd->bhlm', q, mk).astype(jnp.float32) * (MEM_HEAD_DIM ** -0.5)
    p = jax.nn.softmax(s, axis=-1).astype(mv.dtype)
    o = jnp.einsum('bhlm,bmhd->blhd', p, mv).reshape(B, L, D_MODEL)
    return o @ wo


def swiglu(h, w_up, w_down):
    gt, up = jnp.split(h @ w_up, 2, axis=-1)
    return (jax.nn.silu(gt) * up) @ w_down


def setup_inputs(seed: int = 0) -> dict:
    key = jax.random.key(seed)
    ks = iter(jax.random.split(key, 80))
    f32 = jnp.float32

    def nrm(shape, scale):
        return scale * jax.random.normal(next(ks), shape, f32)

    def unif(shape, lo, hi):
        return jax.random.uniform(next(ks), shape, f32, lo, hi)

    def gain(shape):
        return 1.0 + nrm(shape, 0.01)

    Dn = DEPTH
    a_c = unif((Dn, LRU_DIM), 0.9, 0.999)
    a_base = a_c ** (1.0 / LRU_C)
    lru_lambda = jnp.log(a_base) - jnp.log1p(-a_base)
    s5_a_im = jnp.pi * jnp.broadcast_to(jnp.arange(S5_STATE, dtype=f32), (Dn, S5_GROUPS, S5_STATE))
    return {
        'x_prompt': nrm((BATCH, SEQ, D_MODEL), 1.0),
        'x_sample': nrm((DEC_BATCH, DEC_SEQ, D_MODEL), 1.0),
        'mem_prompt': nrm((BATCH, N_MEM, D_MODEL), 1.0),
        'state_shift': nrm((Dn, DEC_BATCH, D_MODEL), 1.0),
        'state_wkv': nrm((Dn, DEC_BATCH, RWKV_HEADS, RWKV_HEAD, RWKV_HEAD), 0.5),
        'state_conv': nrm((Dn, DEC_BATCH, CONV_W - 1, LRU_DIM), 1.0),
        'state_lru': nrm((Dn, DEC_BATCH, LRU_DIM), 0.5),
        'state_s5_re': nrm((Dn, DEC_BATCH, S5_GROUPS, S5_STATE), 0.3),
        'state_s5_im': nrm((Dn, DEC_BATCH, S5_GROUPS, S5_STATE), 0.3),
        'cache_mem_k': nrm((Dn, DEC_BATCH, N_MEM, MEM_HEADS, MEM_HEAD_DIM), 1.0),
        'cache_mem_v': nrm((Dn, DEC_BATCH, N_MEM, MEM_HEADS, MEM_HEAD_DIM), 1.0),
        'norm_mix': gain((Dn, D_MODEL)),
        'w_in': nrm((Dn, D_MODEL, D_IN), D_MODEL ** -0.5),
        'w_out': nrm((Dn, D_MIX, D_MODEL), 0.5 * D_MIX ** -0.5),
        'mu_rkv': unif((Dn, 3 * RWKV_DIM), 0.0, 1.0),
        'mu_wag': unif((Dn, 3, D_MODEL), 0.0, 1.0),
        'mu_v': unif((Dn - 1, D_MODEL), 0.0, 1.0),
        'w0': unif((Dn, RWKV_DIM), -6.0, -1.0),
        'w1': nrm((Dn, D_MODEL, LORA_DECAY), D_MODEL ** -0.5),
        'w2': nrm((Dn, LORA_DECAY, RWKV_DIM), 0.1 * LORA_DECAY ** -0.5),
        'a0': nrm((Dn, RWKV_DIM), 0.1),
        'a1': nrm((Dn, D_MODEL, LORA_A), D_MODEL ** -0.5),
        'a2': nrm((Dn, LORA_A, RWKV_DIM), 0.1 * LORA_A ** -0.5),
        'v0': nrm((Dn - 1, RWKV_DIM), 0.1),
        'v1': nrm((Dn - 1, D_MODEL, LORA_MV), D_MODEL ** -0.5),
        'v2': nrm((Dn - 1, LORA_MV, RWKV_DIM), 0.1 * LORA_MV ** -0.5),
        'g1': nrm((Dn, D_MODEL, LORA_GATE), D_MODEL ** -0.5),
        'g2': nrm((Dn, LORA_GATE, RWKV_DIM), LORA_GATE ** -0.5),
        'k_k': 0.85 + nrm((Dn, RWKV_DIM), 0.05),
        'k_a': 1.0 + nrm((Dn, RWKV_DIM), 0.05),
        'r_k': nrm((Dn, RWKV_HEADS, RWKV_HEAD), 0.1),
        'ln_x_w': gain((Dn, RWKV_DIM)),
        'ln_x_b': nrm((Dn, RWKV_DIM), 0.01),
        'conv_w': nrm((Dn, CONV_W, LRU_DIM), CONV_W ** -0.5),
        'conv_b': nrm((Dn, LRU_DIM), 0.01),
        'lru_wa': nrm((Dn, LRU_BLOCKS, LRU_BLOCK, LRU_BLOCK), LRU_BLOCK ** -0.5),
        'lru_ba': nrm((Dn, LRU_DIM), 0.01),
        'lru_wi': nrm((Dn, LRU_BLOCKS, LRU_BLOCK, LRU_BLOCK), LRU_BLOCK ** -0.5),
        'lru_bi': nrm((Dn, LRU_DIM), 0.01),
        'lru_lambda': lru_lambda,
        's5_a_re': -0.5 + nrm((Dn, S5_GROUPS, S5_STATE), 0.01),
        's5_a_im': s5_a_im + nrm((Dn, S5_GROUPS, S5_STATE), 0.01),
        's5_log_dt': unif((Dn, S5_GROUPS), math.log(0.001), math.log(0.1)),
        's5_b_re': nrm((Dn, S5_GROUPS, S5_STATE, S5_GROUP), (2 * S5_GROUP) ** -0.5),
        's5_b_im': nrm((Dn, S5_GROUPS, S5_STATE, S5_GROUP), (2 * S5_GROUP) ** -0.5),
        's5_c_re': nrm((Dn, S5_GROUPS, S5_GROUP, S5_STATE), (2 * S5_STATE) ** -0.5),
        's5_c_im': nrm((Dn, S5_GROUPS, S5_GROUP, S5_STATE), (2 * S5_STATE) ** -0.5),
        's5_d': nrm((Dn, S5_DIM), 0.5),
        's5_w_glu': nrm((Dn, S5_DIM, 2 * S5_DIM), S5_DIM ** -0.5),
        's5_b_glu': nrm((Dn, 2 * S5_DIM), 0.01),
        'norm_mem_q': gain((Dn, D_MODEL)),
        'norm_mem_kv': gain((Dn, D_MODEL)),
        'mem_wq': nrm((Dn, D_MODEL, D_MODEL), D_MODEL ** -0.5),
        'mem_wk': nrm((Dn, D_MODEL, D_MODEL), D_MODEL ** -0.5),
        'mem_wv': nrm((Dn, D_MODEL, D_MODEL), D_MODEL ** -0.5),
        'mem_wo': nrm((Dn, D_MODEL, D_MODEL), 0.5 * D_MODEL ** -0.5),
        'norm_ffn': gain((Dn, D_MODEL)),
        'ffn_w_up': nrm((Dn, D_MODEL, 2 * D_FF), D_MODEL ** -0.5),
        'ffn_w_down': nrm((Dn, D_FF, D_MODEL), 0.5 * D_FF ** -0.5),
        'norm_final': gain((D_MODEL,)),
    }


def reference(x_prompt, x_sample, mem_prompt, state_shift, state_wkv, state_conv, state_lru,
              state_s5_re, state_s5_im, cache_mem_k, cache_mem_v,
              norm_mix, w_in, w_out, mu_rkv, mu_wag, mu_v, w0, w1, w2, a0, a1, a2, v0, v1, v2,
              g1, g2, k_k, k_a, r_k, ln_x_w, ln_x_b, conv_w, conv_b, lru_wa, lru_ba, lru_wi, lru_bi,
              lru_lambda, s5_a_re, s5_a_im, s5_log_dt, s5_b_re, s5_b_im, s5_c_re, s5_c_im, s5_d,
              s5_w_glu, s5_b_glu, norm_mem_q, norm_mem_kv, mem_wq, mem_wk, mem_wv, mem_wo,
              norm_ffn, ffn_w_up, ffn_w_down, norm_final):
    R3 = 3 * RWKV_DIM

    def run_group(x, mem_ks, mem_vs, shift0, wkv0, conv0, lru0, re0, im0, reset_first):
        sh_l, wkv_l, conv_l, lru_l, re_l, im_l = [], [], [], [], [], []
        v_first = None
        for l in range(DEPTH):
            h = rms_norm(x, norm_mix[l])
            z = h @ w_in[l]
            z_rkv = z[..., :R3]
            z_lx = z[..., R3:R3 + LRU_DIM]
            z_lg = z[..., R3 + LRU_DIM:R3 + 2 * LRU_DIM]
            z_s5 = z[..., R3 + 2 * LRU_DIM:]
            vres = None if l == 0 else (mu_v[l - 1], v0[l - 1], v1[l - 1], v2[l - 1])
            y_a, wkv_new, v_first = rwkv7_time_mix(
                h, shift0[l], z_rkv, w_in[l][:, :R3], wkv0[l], v_first, vres, mu_rkv[l], mu_wag[l],
                w0[l], w1[l], w2[l], a0[l], a1[l], a2[l], g1[l], g2[l], k_k[l], k_a[l], r_k[l],
                ln_x_w[l], ln_x_b[l])
            y_b, conv_new, lru_new = rg_lru_branch(
                z_lx, z_lg, conv0[l], lru0[l], conv_w[l], conv_b[l], lru_wa[l], lru_ba[l],
                lru_wi[l], lru_bi[l], lru_lambda[l], reset_first)
            y_c, re_new, im_new = s5_branch(
                z_s5, re0[l], im0[l], s5_a_re[l], s5_a_im[l], s5_log_dt[l], s5_b_re[l], s5_b_im[l],
                s5_c_re[l], s5_c_im[l], s5_d[l], s5_w_glu[l], s5_b_glu[l])
            x = x + jnp.concatenate([y_a, y_b, y_c], axis=-1) @ w_out[l]
            x = x + mem_attend(rms_norm(x, norm_mem_q[l]), mem_ks[l], mem_vs[l], mem_wq[l], mem_wo[l])
            x = x + swiglu(rms_norm(x, norm_ffn[l]), ffn_w_up[l], ffn_w_down[l])
            sh_l.append(h[:, -1])
            wkv_l.append(wkv_new)
            conv_l.append(conv_new)
            lru_l.append(lru_new)
            re_l.append(re_new)
            im_l.append(im_new)
        return (rms_norm(x, norm_final), jnp.stack(sh_l), jnp.stack(wkv_l), jnp.stack(conv_l),
                jnp.stack(lru_l), jnp.stack(re_l), jnp.stack(im_l))

    Bp = x_prompt.shape[0]
    f32 = jnp.float32
    mem_kv = [mem_project(mem_prompt, norm_mem_kv[l], mem_wk[l], mem_wv[l]) for l in range(DEPTH)]
    mem_k_p = jnp.stack([kv[0] for kv in mem_kv])
    mem_v_p = jnp.stack([kv[1] for kv in mem_kv])
    (y_prompt, shift_p, wkv_p, conv_p, lru_p, re_p, im_p) = run_group(
        x_prompt, mem_k_p, mem_v_p,
        jnp.zeros((DEPTH, Bp, D_MODEL), x_prompt.dtype),
        jnp.zeros((DEPTH, Bp, RWKV_HEADS, RWKV_HEAD, RWKV_HEAD), f32),
        jnp.zeros((DEPTH, Bp, CONV_W - 1, LRU_DIM), x_prompt.dtype),
        jnp.zeros((DEPTH, Bp, LRU_DIM), f32),
        jnp.zeros((DEPTH, Bp, S5_GROUPS, S5_STATE), f32),
        jnp.zeros((DEPTH, Bp, S5_GROUPS, S5_STATE), f32),
        True)
    (y_sample, shift_s, wkv_s, conv_s, lru_s, re_s, im_s) = run_group(
        x_sample, cache_mem_k, cache_mem_v, state_shift, state_wkv, state_conv, state_lru,
        state_s5_re, state_s5_im, False)
    return (y_prompt, y_sample, shift_p, shift_s, wkv_p, wkv_s, conv_p, conv_s, lru_p, lru_s,
            re_p, re_s, im_p, im_s, mem_k_p, mem_v_p)
```

```python
import os
import numpy as np
import concourse.bass as bass
import concourse.mybir as mybir
from concourse.bass_utils import run_bass_kernel_spmd
from contextlib import ExitStack

F32 = mybir.dt.float32
BF16 = mybir.dt.bfloat16
I32 = mybir.dt.int32
AF = mybir.ActivationFunctionType
ALU = mybir.AluOpType
AX = mybir.AxisListType

NCORES = 8
D = 1024
SEQ = 2048
TB = 512
NPB = SEQ // TB
NSS = 16
LS = 8
NMEM = 256
DFF = 2816
DIN = 2304
STAB = 32
LSUB = 5
RWL = int(os.environ.get('RWL', '9'))


class Res:
    __slots__ = ("lw", "rd")

    def __init__(self):
        self.lw = None
        self.rd = []


class Op:
    __slots__ = ("eng", "fn", "deps", "is_dma", "sem", "val", "needed", "slot", "tag")


_TAG = ["-"]
_NAMES = {}


class Prog:
    ENGS = ("pe", "act", "dve", "pool", "sp")

    def __init__(self, nc, n_dma_sems=16):
        self.nc = nc
        self.ops = []
        self.by_eng = {e: [] for e in self.ENGS}
        self.n_dma_sems = n_dma_sems
        self.dma_count = {e: 0 for e in self.ENGS}

    def add(self, eng, fn, reads=(), writes=(), dma=False):
        op = Op()
        op.eng = eng
        op.fn = fn
        op.is_dma = dma
        op.tag = _TAG[0]
        deps = set()
        for r in reads:
            if r.lw is not None:
                deps.add(r.lw)
        for w in writes:
            if w.lw is not None:
                deps.add(w.lw)
            for rr in w.rd:
                deps.add(rr)
        for r in reads:
            r.rd.append(op)
        for w in writes:
            w.lw = op
            w.rd = []
        deps.discard(op)
        op.deps = deps
        op.needed = False
        op.sem = None
        op.val = 0
        op.slot = 0
        if dma:
            k = self.dma_count[eng]
            self.dma_count[eng] = k + 1
            op.slot = k % self.n_dma_sems
            op.val = 16 * (k // self.n_dma_sems + 1)
        self.ops.append(op)
        self.by_eng[eng].append(op)
        return op

    def emit(self, stack):
        nc = self.nc
        engobj = {"pe": "tensor", "act": "scalar", "dve": "vector", "pool": "gpsimd", "sp": "sync"}
        for op in self.ops:
            for d in op.deps:
                if d.is_dma:
                    continue
                if d.eng == "pe" and op.eng == "pe" and not op.is_dma:
                    continue
                d.needed = True
        EPOCH = 3000
        for e in ("pe", "act", "dve", "pool"):
            sem = None
            cnt = EPOCH
            nep = 0
            for op in self.by_eng[e]:
                if not op.is_dma and op.needed:
                    if cnt >= EPOCH:
                        sem = stack.enter_context(nc.semaphore("s_%s_%d" % (e, nep)))
                        nep += 1
                        cnt = 0
                    cnt += 1
                    op.val = cnt
                    op.sem = sem
        for e in self.ENGS:
            if self.dma_count[e] > 0:
                sems = [stack.enter_context(nc.semaphore("d_%s_%d" % (e, i)))
                        for i in range(min(self.n_dma_sems, self.dma_count[e]))]
                for op in self.by_eng[e]:
                    if op.is_dma:
                        op.sem = sems[op.slot]
        final_dma = {}
        for op in self.ops:
            if op.is_dma:
                final_dma[id(op.sem)] = (op.sem, op.val)
        block = stack.enter_context(nc.Block())
        prog = self

        def make(ename):
            def body(eng):
                waited = {}

                def wait(sem, val):
                    if waited.get(id(sem), 0) < val:
                        eng.wait_ge(sem, val)
                        waited[id(sem)] = val

                for op in prog.by_eng[ename]:
                    for d in op.deps:
                        if (not d.is_dma) and d.eng == "pe" and ename == "pe" and not op.is_dma:
                            continue
                        wait(d.sem, d.val)
                    if op.is_dma:
                        if op.val > 16:
                            wait(op.sem, op.val - 16)
                        ins = op.fn(eng)
                        _NAMES[ins.ins.name] = op.tag
                        ins.then_inc(op.sem, 16)
                    else:
                        ins = op.fn(eng)
                        _NAMES[ins.ins.name] = op.tag
                        if op.needed:
                            ins.then_inc(op.sem, 1)
                if ename == "sp":
                    for sem, val in final_dma.values():
                        wait(sem, val)
            return body

        for e in self.ENGS:
            if len(self.by_eng[e]) == 0 and e != "sp":
                continue
            getattr(block, engobj[e])(make(e))


class V:
    __slots__ = ("ap", "res", "allres")

    def __init__(self, ap, res, allres=None):
        self.ap = ap
        self.res = res
        self.allres = allres

    def __getitem__(self, key):
        return V(self.ap[key], self.res, self.allres)

    def rr(self, s, **kw):
        return V(self.ap.rearrange(s, **kw), self.res, self.allres)

    def bc(self, shape):
        return V(self.ap.to_broadcast(list(shape)), self.res, self.allres)

    def bitcast(self, dt):
        return V(self.ap.bitcast(dt), self.res, self.allres)


class T:
    def __init__(self, tensor, nres=1):
        self.t = tensor
        self.res = [Res() for _ in range(nres)]

    def __getitem__(self, key):
        ap = self.t[key]
        if len(self.res) > 1 and isinstance(key, tuple) and len(key) >= 2:
            k = key[1]
            if isinstance(k, int):
                return V(ap, [self.res[k]])
            if isinstance(k, slice):
                return V(ap, self.res[k])
        return V(ap, list(self.res))


class TP(T):
    def __init__(self, tensor):
        self.t = tensor
        self.res = [Res() for _ in range(8)]

    def __getitem__(self, key):
        ap = self.t[key]
        cs = key[1]
        c0 = cs.start or 0
        c1 = 512 if cs.stop is None else cs.stop
        return V(ap, self.res[c0 // 64:(c1 - 1) // 64 + 1], self.res)


def _a(x):
    return x.ap if isinstance(x, V) else x


def _r(*xs):
    out = []
    for x in xs:
        if isinstance(x, V):
            out += x.res
    return out


PV_SPEC = [("norm_mix", 8), ("norm_mem_q", 8), ("norm_ffn", 8), ("norm_mem_kv", 8), ("norm_final", 8),
           ("mu_r", 4), ("mu_k", 4), ("mu_v", 4), ("mu_w", 8), ("mu_a", 8), ("mu_g", 8), ("mu_vres", 8),
           ("w0", 4), ("a0", 4), ("v0", 4), ("k_k", 4), ("k_a", 4), ("r_k", 4), ("ln_w", 4), ("ln_b", 4),
           ("conv_w0", 2), ("conv_w1", 2), ("conv_w2", 2), ("conv_w3", 2), ("conv_b", 2),
           ("lru_ba", 2), ("lru_bi", 2), ("lru_lambda", 2),
           ("s5_are", 8), ("s5_aim", 8), ("s5_logdt", 8), ("s5_d", 2), ("b_glu", 4)]
PV_OFF = {}
_o = 0
for _n, _k in PV_SPEC:
    PV_OFF[_n] = _o
    _o += _k
NPV = _o


def _cols(v):
    v = np.asarray(v, np.float32).reshape(-1)
    return v.reshape(-1, 128).T


def pack_pvec(inp, l):
    pv = np.zeros((128, NPV), np.float32)

    def put(name, v):
        c = _cols(v)
        pv[:, PV_OFF[name]:PV_OFF[name] + c.shape[1]] = c
    put("norm_mix", inp["norm_mix"][l])
    put("norm_mem_q", inp["norm_mem_q"][l])
    put("norm_ffn", inp["norm_ffn"][l])
    put("norm_mem_kv", inp["norm_mem_kv"][l])
    put("norm_final", inp["norm_final"])
    mu = inp["mu_rkv"][l]
    put("mu_r", mu[0:512])
    put("mu_k", mu[512:1024])
    put("mu_v", mu[1024:1536])
    put("mu_w", inp["mu_wag"][l][0])
    put("mu_a", inp["mu_wag"][l][1])
    put("mu_g", inp["mu_wag"][l][2])
    if l >= 1:
        put("mu_vres", inp["mu_v"][l - 1])
        put("v0", inp["v0"][l - 1])
    put("w0", inp["w0"][l])
    put("a0", inp["a0"][l])
    put("k_k", inp["k_k"][l])
    put("k_a", inp["k_a"][l])
    put("r_k", inp["r_k"][l])
    put("ln_w", inp["ln_x_w"][l])
    put("ln_b", inp["ln_x_b"][l])
    for j in range(4):
        put("conv_w%d" % j, inp["conv_w"][l][j])
    put("conv_b", inp["conv_b"][l])
    put("lru_ba", inp["lru_ba"][l])
    put("lru_bi", inp["lru_bi"][l])
    put("lru_lambda", inp["lru_lambda"][l])
    def chan(a):
        a = np.asarray(a, np.float32).reshape(8, 2, 64)
        return a.transpose(1, 2, 0).reshape(128, 8)
    pv[:, PV_OFF["s5_are"]:PV_OFF["s5_are"] + 8] = chan(inp["s5_a_re"][l])
    pv[:, PV_OFF["s5_aim"]:PV_OFF["s5_aim"] + 8] = chan(inp["s5_a_im"][l])
    pv[:, PV_OFF["s5_logdt"]:PV_OFF["s5_logdt"] + 8] = chan(np.repeat(np.asarray(inp["s5_log_dt"][l])[:, None], 64, 1))
    put("s5_d", inp["s5_d"][l])
    put("b_glu", inp["s5_b_glu"][l])
    return pv


def s5_blocks(inp, l):
    b_re = np.asarray(inp["s5_b_re"][l], np.float32)
    b_im = np.asarray(inp["s5_b_im"][l], np.float32)
    c_re = np.asarray(inp["s5_c_re"][l], np.float32)
    c_im = np.asarray(inp["s5_c_im"][l], np.float32)
    Bb = np.zeros((2, 128, 8, 128), np.float32)
    Cb = np.zeros((2, 128, 8, 128), np.float32)
    for g in range(16):
        ct, gg, g8 = g // 2, g % 2, g % 8
        Bb[0, g8 * 16:(g8 + 1) * 16, ct, gg * 64:(gg + 1) * 64] = b_re[g].T
        Bb[1, g8 * 16:(g8 + 1) * 16, ct, gg * 64:(gg + 1) * 64] = b_im[g].T
        Cb[0, gg * 64:(gg + 1) * 64, ct, g8 * 16:(g8 + 1) * 16] = c_re[g].T
        Cb[1, gg * 64:(gg + 1) * 64, ct, g8 * 16:(g8 + 1) * 16] = c_im[g].T
    return Bb, Cb


def lru_blocks(w):
    w = np.asarray(w, np.float32)
    o = np.zeros((128, 2, 128), np.float32)
    for h in range(4):
        t, hh = h // 2, h % 2
        o[hh * 64:(hh + 1) * 64, t, hh * 64:(hh + 1) * 64] = w[h]
    return o


class Bld:
    def __init__(self, nc, st):
        self.nc = nc
        self.st = st
        self.P = Prog(nc)
        self.taps = {}
        self.flip = 0

    def sb(self, name, shape, dt, nres=1):
        return T(self.st.enter_context(self.nc.sbuf_tensor(name, list(shape), dt)), nres)

    def psum(self, name, shape, dt):
        return T(self.st.enter_context(self.nc.psum_tensor(name, list(shape), dt)), 1)

    def din(self, name, shape, dt=F32):
        return T(self.nc.dram_tensor(name, list(shape), dt, kind="ExternalInput").ap(), 1)

    def dout(self, name, shape, dt=F32):
        return T(self.nc.dram_tensor(name, list(shape), dt, kind="ExternalOutput").ap(), 1)

    def tt(self, out, a, b, op, eng="dve"):
        self.P.add(eng, lambda e: e.tensor_tensor(out=_a(out), in0=_a(a), in1=_a(b), op=op), _r(a, b), _r(out))

    def ts(self, out, a, s1, s2, op0, op1=None, eng="dve"):
        if op1 is None:
            self.P.add(eng, lambda e: e.tensor_scalar(out=_a(out), in0=_a(a), scalar1=_a(s1), scalar2=None, op0=op0),
                       _r(a, s1), _r(out))
        else:
            self.P.add(eng, lambda e: e.tensor_scalar(out=_a(out), in0=_a(a), scalar1=_a(s1), scalar2=_a(s2), op0=op0, op1=op1),
                       _r(a, s1, s2), _r(out))

    def stt(self, out, a, s, b, op0, op1):
        self.P.add("dve", lambda e: e.scalar_tensor_tensor(out=_a(out), in0=_a(a), scalar=_a(s), in1=_a(b), op0=op0, op1=op1),
                   _r(a, s, b), _r(out))

    def act(self, out, in_, func, bias=None, scale=None):
        kw = {}
        if bias is not None:
            kw["bias"] = _a(bias)
        if scale is not None:
            kw["scale"] = _a(scale)
        self.P.add("act", lambda e: e.activation(out=_a(out), in_=_a(in_), func=func, **kw), _r(in_, bias, scale), _r(out))

    def cp(self, out, in_, eng="dve"):
        if eng == "act":
            self.P.add("act", lambda e: e.copy(out=_a(out), in_=_a(in_)), _r(in_), _r(out))
        else:
            self.P.add(eng, lambda e: e.tensor_copy(out=_a(out), in_=_a(in_)), _r(in_), _r(out))

    def evac(self, out, in_):
        self.flip ^= 1
        self.cp(out, in_, "act" if self.flip else "dve")

    def mm(self, out, lhsT, rhs, start=True, stop=True):
        ob = _a(out).base_partition()
        lb = _a(lhsT).base_partition()
        kw = {}
        if ob != lb:
            kw["tile_position"] = (lb, ob)
        wr = out.allres if out.allres is not None else out.res
        self.P.add("pe", lambda e: e.matmul(_a(out), lhsT=_a(lhsT), rhs=_a(rhs), start=start, stop=stop, **kw),
                   _r(lhsT, rhs), wr)

    def tr(self, out, in_, ident):
        self.P.add("pe", lambda e: e.transpose(_a(out), _a(in_), _a(ident)), _r(in_, ident), _r(out))

    def scan(self, out, d0, d1, init):
        self.P.add("dve", lambda e: e.tensor_tensor_scan(out=_a(out), data0=_a(d0), data1=_a(d1), initial=_a(init),
                                                         op0=ALU.mult, op1=ALU.add), _r(d0, d1, init), _r(out))

    def recip(self, out, in_):
        self.P.add("dve", lambda e: e.reciprocal(out=_a(out), in_=_a(in_)), _r(in_), _r(out))

    def memset(self, v, val, eng="pool"):
        self.P.add(eng, lambda e: e.memset(_a(v), val), (), _r(v))

    def asel(self, v, pattern, op, cm, base=0):
        self.P.add("pool", lambda e: e.affine_select(out=_a(v), in_=_a(v), pattern=pattern, compare_op=op, fill=0.0,
                                                     base=base, channel_multiplier=cm), _r(v), _r(v))

    def rmax(self, out, in_):
        self.P.add("dve", lambda e: e.reduce_max(out=_a(out), in_=_a(in_), axis=AX.X), _r(in_), _r(out))

    def rsum(self, out, in_):
        self.P.add("dve", lambda e: e.reduce_sum(out=_a(out), in_=_a(in_), axis=AX.X), _r(in_), _r(out))

    def dma(self, out, in_, q="sp"):
        self.P.add(q, lambda e: e.dma_start(out=_a(out), in_=_a(in_)), _r(in_), _r(out), dma=True)

    def tap(self, name, v, shape, dt=F32):
        d = self.dout("dbg_" + name, shape, dt)
        self.dma(d[:], v)
        self.taps[name] = shape


def build(n_blocks=(0, 1, 2, 3, 4), n_layers=2, taps=(), do_rwkv=True, do_lru=True, do_s5=True, do_attn=True, do_ffn=True):
    nc = bass.Bass("TRN2", target_bir_lowering=False)
    st = ExitStack()
    with st:
        b = Bld(nc, st)
        _build(b, n_blocks, n_layers, set(taps), do_rwkv, do_lru, do_s5, do_attn, do_ffn)
        b.P.emit(st)
    return nc, b.taps


def _build(b, n_blocks, n_layers, taps, do_rwkv, do_lru, do_s5, do_attn, do_ffn):
    E = TB + 1
    xp = b.din("xp", [D, SEQ]); xs = b.din("xs", [D, 128]); mem = b.din("mem", [D, NMEM])
    sshift = b.din("sshift", [2, D, NSS]); swkv = b.din("swkv", [2, 128, NSS * 256])
    sconv = b.din("sconv", [2, 256, NSS * 3]); slru = b.din("slru", [2, 256, NSS])
    ss5 = [b.din("ss5re", [2, 128, 8 * NSS]), b.din("ss5im", [2, 128, 8 * NSS])]
    ck = b.din("ck", [2, NSS, 128, 2048]); cv = b.din("cv", [2, NSS, 128, 2048])
    w_in = b.din("w_in", [2, D, DIN]); w_out = b.din("w_out", [2, D, D])
    wq = b.din("wq", [2, D, D]); wk = b.din("wk", [2, D, D]); wv = b.din("wv", [2, D, D]); wo = b.din("wo", [2, D, D])
    w_up = b.din("w_up", [2, D, 2 * DFF]); w_down = b.din("w_down", [2, DFF, D]); w_glu = b.din("w_glu", [2, 256, 512])
    w1 = b.din("w1", [2, D, 64]); a1 = b.din("a1", [2, D, 64]); v1 = b.din("v1", [1, D, 32]); g1 = b.din("g1", [2, D, 160])
    w2 = b.din("w2", [2, 64, 512]); a2 = b.din("a2", [2, 64, 512]); v2 = b.din("v2", [1, 32, 512]); g2 = b.din("g2", [2, 160, 512])
    pvd = b.din("pvec", [2, 128, NPV]); s5B = b.din("s5B", [2, 2, 128, 1024]); s5C = b.din("s5C", [2, 2, 128, 1024])
    lruA = b.din("lruA", [2, 128, 256]); lruI = b.din("lruI", [2, 128, 256])
    yp = b.dout("yp", [D, SEQ]); ys = b.dout("ys", [D, 128])
    shp = b.dout("shp", [2, 128, 8]); shs = b.dout("shs", [2, 128, 8 * NSS])
    wkvp = b.dout("wkvp", [2, 128, 256]); wkvs = b.dout("wkvs", [2, 128, NSS * 256])
    convp = b.dout("convp", [2, 128, 6]); convs = b.dout("convs", [2, 128, 2 * NSS * 3])
    lrup = b.dout("lrup", [2, 2, 128]); lrus = b.dout("lrus", [2, 128, 2 * NSS])
    s5p = [b.dout("s5rep", [2, 128, 8]), b.dout("s5imp", [2, 128, 8])]
    s5s = [b.dout("s5res", [2, 128, 8 * NSS]), b.dout("s5ims", [2, 128, 8 * NSS])]
    memk = b.dout("memk", [2, NMEM, D]); memv = b.dout("memv", [2, NMEM, D])

    ident = b.sb("ident", [128, 128], BF16)
    onesf = b.sb("onesf", [128, 128], F32); blk1 = b.sb("blk1", [128, 128], F32)
    m4 = {64: b.sb("m4_64", [64, 2, 4, 64], BF16), 8: b.sb("m4_8", [8, 2, 4, 8], BF16)}
    mT = {64: b.sb("mT_64", [64, 2, 64], BF16), 8: b.sb("mT_8", [8, 2, 8], BF16)}
    rmask = {64: b.sb("rmask64", [128, TB], BF16), 8: b.sb("rmask8", [128, 128], BF16), STAB: b.sb("rmaskS", [128, TB], BF16)}
    pv = b.sb("pv", [128, NPV], F32); pvx = b.sb("pvx", [128, 16], F32)
    w1b = b.sb("w1b", [128, 8, 64], BF16); a1b = b.sb("a1b", [128, 8, 64], BF16)
    v1b = b.sb("v1b", [128, 8, 32], BF16); g1b = b.sb("g1b", [128, 8, 160], BF16)
    w2b = b.sb("w2b", [64, 512], BF16); a2b = b.sb("a2b", [64, 512], BF16); v2b = b.sb("v2b", [32, 512], BF16)
    g2b = b.sb("g2b", [128, 2, 512], BF16)
    Bb = b.sb("Bb", [128, 2, 8, 128], BF16); Cb = b.sb("Cb", [128, 2, 8, 128], BF16)
    lab = b.sb("lab", [128, 2, 128], BF16); lib = b.sb("lib", [128, 2, 128], BF16)
    tabs = b.sb("tabs", [128, 8, 4, STAB], F32); s5c = b.sb("s5c", [128, 12, 8], F32); s5t = b.sb("s5t", [128, 8, 8], F32)
    s5u = b.sb("s5u", [128, 8, 2, 10], F32); rhot = b.sb("rhot", [128, 8, STAB], F32)
    rpow = b.sb("rpow", [128, 8, STAB], F32); gtab = b.sb("gtab", [128, 8, 2, 16], F32); iot = b.sb("iot", [128, STAB], F32)
    hlast = b.sb("hlast", [128, 2, 8, 1], F32); zlast = b.sb("zlast", [128, 2, 12, 1], F32)
    clast = b.sb("clast", [128, 2, 2, 3], F32); hlru = b.sb("hlru", [128, 2, 2, 1], F32)
    s5st = b.sb("s5st", [128, 2, 2, 8], F32); ST = b.sb("ST", [128, 2, 4, 64], F32)
    kT = b.sb("kT", [128, 2, 8, NMEM], BF16); vM = b.sb("vM", [128, 2, 2, D], BF16)
    xT = b.sb("xT", [128, 8, TB], F32, 8); vfirst = b.sb("vfirst", [128, 4, TB], BF16, 4)
    arena = b.sb("arena", [128, 16, E], F32, 16)
    hxb = b.sb("hxb", [128, 8, E], BF16, 8); mx = b.sb("mx", [128, 8, TB], BF16, 8)
    zrkv = b.sb("zrkv", [128, 12, E], F32, 12)
    zlx = b.sb("zlx", [128, 2, TB + 3 * NSS], F32, 2); zlg = b.sb("zlg", [128, 2, TB], F32, 2)
    zs5 = b.sb("zs5", [128, 2, TB], F32, 2); zs5b = b.sb("zs5b", [128, 2, TB], BF16, 2)
    tl = b.sb("tl", [128, 5, TB], BF16, 5); bR = b.sb("bR", [128, 8, TB], BF16, 8)
    NTG = b.sb("NTG", [64, 2, 256], BF16, 2); PbG = b.sb("PbG", [64, 4, 256], BF16, 4); IPG = b.sb("IPG", [64, 2, 256], BF16, 2)
    XbG = b.sb("XbG", [64, 4, 256], BF16, 4); Z2f = b.sb("Z2f", [64, 2, 256], F32, 2); Z1c = b.sb("Z1c", [64, 256], BF16)
    Z1T = b.sb("Z1T", [128, 2, 128], BF16, 2); KVf = b.sb("KVf", [128, 2, 128], F32, 2)
    Stmp = b.sb("Stmp", [128, 2, 64], F32, 2); Xb = b.sb("Xb", [64, 2, 128], BF16, 2); STb = b.sb("STb", [128, 2, 64], BF16, 2)
    gC = b.sb("gC", [128, 16], F32); sw = b.sb("sw", [128, 4, 64], F32, 4)
    ycat = b.sb("ycat", [128, 8, TB], BF16, 8); qb = mx
    PTb = b.sb("PTb", [128, 2, TB], BF16, 2); Pn = b.sb("Pn", [128, 2, NMEM], BF16, 2); sm = b.sb("sm", [128, 8], F32)
    hid2 = T(zrkv.t.bitcast(BF16), 1); hid2.res = zrkv.res
    wp = b.sb("wp", [128, 2, 2816], BF16, 2)
    ckb = T(zlg.t.bitcast(BF16), 1); ckb.res = zlg.res; cvb = T(zs5.t.bitcast(BF16), 1); cvb.res = zs5.res
    arb = T(arena.t.bitcast(BF16), 1); arb.res = arena.res
    stg = b.sb("stg", [128, 8 * NSS], F32); stg2 = b.sb("stg2", [128, 8 * NSS], F32)
    pb = [TP(b.psum("pb%d" % i, [128, 512], F32).t) for i in range(4)]
    ptr = b.psum("ptr", [128, 1024], BF16)
    pg = [b.psum("pg%d" % i, [128, 512], F32) for i in range(3)]
    gi = [0]

    def bank():
        gi[0] = (gi[0] + 1) % 3
        return pg[gi[0]]

    def pvc(name, j, n=1):
        o = PV_OFF[name] + j
        return pv[:, o:o + n]

    b.memset(onesf[:], 1.0)
    b.memset(blk1[:], 0.0)
    b.memset(blk1[0:64, 0:64], 1.0)
    b.memset(blk1[64:128, 64:128], 1.0)
    identf = arena[:, 0, 0:128]
    b.memset(identf, 1.0)
    b.asel(identf, [[-1, 128]], ALU.is_equal, 1)
    b.cp(ident[:], identf)
    for C in (64, 8):
        b.memset(m4[C][:], 1.0)
        b.memset(mT[C][:], 1.0)
        for hh in range(2):
            for blk in range(4):
                b.asel(m4[C][:, hh, blk, :], [[1, C]], ALU.is_gt if blk in (0, 2) else ALU.is_ge, -1)
            b.asel(mT[C][:, hh, :], [[-1, C]], ALU.is_gt, 1)
        b.memset(rmask[C][:], 1.0)
        b.memset(rmask[C][:].rr("p (c k) -> p c k", k=C)[:, :, 0:1], 0.0)
    b.memset(rmask[STAB][:], 1.0)
    b.memset(rmask[STAB][:].rr("p (c k) -> p c k", k=STAB)[:, :, 0:1], 0.0)
    b.scan(iot[:], onesf[:, 0:STAB], onesf[:, 0:STAB], 0.0)
    for t_ in (hlast, zlast, clast, hlru, s5st, ST):
        b.memset(t_[:], 0.0)
    b.memset(tl[:, 4, :], 0.0)
    b.memset(g2b[:, 1, :], 0.0)

    wpi = [0]

    def proj(wv_, nk, M, rhs_fn, N, evac_fn, kparts=None):
        pc = min(M, 512, (2816 // nk) // 128 * 128)
        wr = wv_.rr("(k p) m -> p k m", p=128)
        for c0 in range(0, M, pc):
            i = wpi[0] = wpi[0] ^ 1
            pcc = min(pc, M - c0)
            b.dma(wp[:, i, 0:nk * pcc].rr("p (k m) -> p k m", k=nk), wr[:, :, c0:c0 + pcc], q="pool")
            for m0 in range(0, pcc, 128):
                ps = bank()
                for kt in range(nk):
                    b.mm(ps[:, 0:N], wp[:, i, kt * pcc + m0:kt * pcc + m0 + 128], rhs_fn(kt), kt == 0, kt == nk - 1)
                evac_fn((c0 + m0) // 128, ps)

    def rmsnorm(x_fn, N, gname, out_fn, t0=14):
        ps = bank()
        for kt in range(8):
            sq = arena[:, t0 + (kt % 2), 0:N]
            b.act(sq, x_fn(kt), AF.Square)
            b.mm(ps[:, 0:N], onesf[:], sq, kt == 0, kt == 7)
        rs = arena[:, t0, 0:N]
        b.act(rs, ps[:, 0:N], AF.Sqrt, bias=pvx[:, 4:5], scale=1.0 / D)
        b.recip(rs, rs)
        for kt in range(8):
            b.stt(out_fn(kt), x_fn(kt), pvc(gname, kt), rs, ALU.mult, ALU.mult)

    def mem_phase(l):
        _TAG[0] = 'mem'
        b.dma(pv[:], pvd[l])
        b.memset(pvx[:, 4:5], 1e-6)
        for kt in range(8):
            b.dma(arena[:, kt, 0:NMEM], mem[kt * 128:(kt + 1) * 128, :])
        rmsnorm(lambda kt: arena[:, kt, 0:NMEM], NMEM, "norm_mem_kv", lambda kt: hxb[:, kt, 0:NMEM])
        proj(wk[l], 8, D, lambda kt: hxb[:, kt, 0:NMEM], NMEM, lambda mt, ps: b.evac(kT[:, l, mt, :], ps[:, 0:NMEM]))
        for (wsrc, dst, keep) in ((wk, memk, False), (wv, memv, True)):
            wr = wsrc[l].rr("(k p) m -> p k m", p=128)
            for c0 in range(0, D, 256):
                i = wpi[0] = wpi[0] ^ 1
                b.dma(wp[:, i, 0:2048].rr("p (k m) -> p k m", k=8), wr[:, :, c0:c0 + 256], q="pool")
                for mt in range(2):
                    ps = bank()
                    for kt in range(8):
                        b.mm(ps[:, 0:256], hxb[:, kt, mt * 128:(mt + 1) * 128], wp[:, i, kt * 256:(kt + 1) * 256], kt == 0, kt == 7)
                    o32 = arena[:, 8 + mt, 0:256]
                    b.evac(o32, ps[:, 0:256])
                    b.dma(dst[l, mt * 128:(mt + 1) * 128, c0:c0 + 256], o32)
                    if keep:
                        b.cp(vM[:, l, mt, c0:c0 + 256], o32, "pool")

    def load_layer_consts(l):
        _TAG[0] = 'consts'
        b.dma(pv[:], pvd[l])
        b.dma(w1b[:], w1[l].rr("(k p) m -> p k m", p=128), q="pool")
        b.dma(a1b[:], a1[l].rr("(k p) m -> p k m", p=128), q="pool")
        b.dma(g1b[:], g1[l].rr("(k p) m -> p k m", p=128), q="pool")
        b.dma(w2b[:], w2[l], q="pool")
        b.dma(a2b[:], a2[l], q="pool")
        b.dma(g2b[:, 0, :], g2[l, 0:128, :], q="pool")
        b.dma(g2b[0:32, 1, :], g2[l, 128:160, :], q="pool")
        if l >= 1:
            b.dma(v1b[:], v1[l - 1].rr("(k p) m -> p k m", p=128), q="pool")
            b.dma(v2b[:], v2[l - 1], q="pool")
        b.dma(Bb[:].rr("p r c m -> p r (c m)"), s5B[l].rr("r p x -> p r x"), q="pool")
        b.dma(Cb[:].rr("p r c m -> p r (c m)"), s5C[l].rr("r p x -> p r x"), q="pool")
        b.ts(Cb[:, 1], Cb[:, 1], -1.0, None, ALU.mult, eng="pool")
        b.dma(lab[:].rr("p t m -> p (t m)"), lruA[l], q="pool")
        b.dma(lib[:].rr("p t m -> p (t m)"), lruI[l], q="pool")
        b.memset(pvx[:, 4:5], 1e-6)
        b.memset(pvx[:, 5:6], 64e-5)
        b.ts(pvx[:, 0:4], pvc("k_a", 0, 4), -1.0, 1.0, ALU.mult, ALU.add)
        b.act(pvx[:, 6:8], pvc("lru_lambda", 0, 2), AF.Sigmoid)
        b.act(pvx[:, 6:8], pvx[:, 6:8], AF.Ln)
        b.ts(pvx[:, 8:10], pvx[:, 6:8], 16.0, None, ALU.mult)
        b.ts(pvx[:, 6:8], pvx[:, 6:8], 8.0, None, ALU.mult)
        if do_s5:
            s5_tables()

    def s5_sincos(dst_s, theta, shift):
        u = s5t[:, 0, :]; ui = s5t[:, 1, :].bitcast(I32); uf = s5t[:, 2, :]; ng = s5t[:, 3, :]
        b.ts(u, theta, 1.0 / (2 * np.pi), shift, ALU.mult, ALU.add)
        b.cp(ui, u)
        b.cp(uf, ui)
        b.tt(u, u, uf, ALU.subtract)
        b.ts(ng, u, 0.0, None, ALU.is_lt)
        b.tt(u, u, ng, ALU.add)
        b.ts(u, u, 2 * np.pi, -np.pi, ALU.mult, ALU.add)
        b.ts(u, u, -3.1415925, 3.1415925, ALU.max, ALU.min)
        b.act(dst_s, u, AF.Sin)

    def s5_tables():
        c = lambda i: s5c[:, i, :]
        b.act(c(0), pvc("s5_logdt", 0, 8), AF.Exp)
        b.tt(c(1), pvc("s5_aim", 0, 8), c(0), ALU.mult)
        b.tt(c(2), pvc("s5_are", 0, 8), c(0), ALU.mult)
        b.act(c(2), c(2), AF.Exp)
        s5_sincos(c(4), c(1), 0.5)
        s5_sincos(c(3), c(1), 0.75)
        b.tt(c(7), c(2), c(3), ALU.mult)
        b.ts(c(7), c(7), -1.0, None, ALU.add)
        b.tt(c(8), c(2), c(4), ALU.mult)
        are = pvc("s5_are", 0, 8); aim = pvc("s5_aim", 0, 8)
        b.tt(c(5), c(7), are, ALU.mult)
        b.tt(c(11), c(8), aim, ALU.mult)
        b.tt(c(5), c(5), c(11), ALU.add)
        b.tt(c(6), c(8), are, ALU.mult)
        b.tt(c(11), c(7), aim, ALU.mult)
        b.tt(c(6), c(6), c(11), ALU.subtract)
        b.tt(c(7), are, are, ALU.mult)
        b.tt(c(8), aim, aim, ALU.mult)
        b.tt(c(7), c(7), c(8), ALU.add)
        b.recip(c(7), c(7))
        b.tt(c(5), c(5), c(7), ALU.mult)
        b.tt(c(6), c(6), c(7), ALU.mult)
        Ec = tabs[:, :, 0, :]; Es = tabs[:, :, 1, :]
        b.memset(tabs[:, :, 0, 0:1], 1.0)
        b.memset(tabs[:, :, 1, 0:1], 0.0)
        b.cp(s5u[:, :, 0, 0:1], c(3).rr("p (c o) -> p c o", o=1))
        b.cp(s5u[:, :, 1, 0:1], c(4).rr("p (c o) -> p c o", o=1))
        nlev = int(np.log2(STAB))
        for k in range(9):
            uc = s5u[:, :, 0, k:k + 1]; us = s5u[:, :, 1, k:k + 1]
            if k < nlev:
                n = 1 << k
                ucb = uc.bc([128, 8, n]); usb = us.bc([128, 8, n])
                s5tmp = arena[:, 12, 0:512].rr("p (a c t) -> p a c t", a=4, c=8)
                t1 = s5tmp[:, 0, :, 0:n]; t2 = s5tmp[:, 1, :, 0:n]; t3 = s5tmp[:, 2, :, 0:n]; t4 = s5tmp[:, 3, :, 0:n]
                b.tt(t1, Ec[:, :, 0:n], ucb, ALU.mult)
                b.tt(t2, Es[:, :, 0:n], usb, ALU.mult)
                b.tt(t3, Ec[:, :, 0:n], usb, ALU.mult)
                b.tt(t4, Es[:, :, 0:n], ucb, ALU.mult)
                b.tt(Ec[:, :, n:2 * n], t1, t2, ALU.subtract)
                b.tt(Es[:, :, n:2 * n], t3, t4, ALU.add)
            if k < 9:
                a = s5t[:, 4, :].rr("p (c o) -> p c o", o=1); bb_ = s5t[:, 5, :].rr("p (c o) -> p c o", o=1)
                b.tt(a, uc, uc, ALU.mult)
                b.tt(bb_, us, us, ALU.mult)
                b.tt(s5u[:, :, 0, k + 1:k + 2], a, bb_, ALU.subtract)
                b.tt(a, uc, us, ALU.mult)
                b.ts(s5u[:, :, 1, k + 1:k + 2], a, 2.0, None, ALU.mult)
        fre = c(5).rr("p (c o) -> p c o", o=1).bc([128, 8, STAB]); fim = c(6).rr("p (c o) -> p c o", o=1).bc([128, 8, STAB])
        Fc = tabs[:, :, 2, :]; Fs = tabs[:, :, 3, :]
        t1 = arena[:, 15, 0:8 * STAB].rr("p (c t) -> p c t", t=STAB)
        b.tt(Fc, Ec, fre, ALU.mult)
        b.tt(t1, Es, fim, ALU.mult)
        b.tt(Fc, Fc, t1, ALU.add)
        b.tt(Fs, Ec, fim, ALU.mult)
        b.tt(t1, Es, fre, ALU.mult)
        b.tt(Fs, Fs, t1, ALU.subtract)
        b.cp(rhot[:], c(2).rr("p (c o) -> p c o", o=1).bc([128, 8, STAB]))
        b.tt(c(9), pvc("s5_are", 0, 8), c(0), ALU.mult)
        b.act(c(10), c(9), AF.Exp, scale=float(STAB))
        for ct in range(8):
            b.act(rpow[:, ct, :], iot[:], AF.Exp, scale=s5c[:, 9, ct:ct + 1])
        Gc = gtab[:, :, 0, :]; Gs = gtab[:, :, 1, :]
        b.memset(gtab[:, :, 0, 0:1], 1.0)
        b.memset(gtab[:, :, 1, 0:1], 0.0)
        for k in range(4):
            n = 1 << k
            uc = s5u[:, :, 0, LSUB + k:LSUB + k + 1].bc([128, 8, n]); us = s5u[:, :, 1, LSUB + k:LSUB + k + 1].bc([128, 8, n])
            s5tmp = arena[:, 12, 0:512].rr("p (a c t) -> p a c t", a=4, c=8)
            t1 = s5tmp[:, 0, :, 0:n]; t2 = s5tmp[:, 1, :, 0:n]; t3 = s5tmp[:, 2, :, 0:n]; t4 = s5tmp[:, 3, :, 0:n]
            b.tt(t1, Gc[:, :, 0:n], uc, ALU.mult)
            b.tt(t2, Gs[:, :, 0:n], us, ALU.mult)
            b.tt(t3, Gc[:, :, 0:n], us, ALU.mult)
            b.tt(t4, Gs[:, :, 0:n], uc, ALU.mult)
            b.tt(Gc[:, :, n:2 * n], t1, t2, ALU.subtract)
            b.tt(Gs[:, :, n:2 * n], t3, t4, ALU.add)

    def layer(blk, l):
        nseq, L, N, pbi = blk["nseq"], blk["L"], blk["N"], blk["pbi"]
        smp = pbi is None
        EN = nseq * (L + 1)
        last_p = (not smp) and pbi == NPB - 1

        def ext(tv):
            return tv.rr("p (s t) -> p s t", t=L + 1)

        def sl(tv):
            return tv.rr("p (s t) -> p s t", t=L)
        hx = lambda kt: ext(arena[:, kt, 0:EN])
        hcur = lambda kt: hx(kt)[:, :, 1:L + 1]
        hprev = lambda kt: hx(kt)[:, :, 0:L]
        ar = lambda i: arena[:, i, 0:N]

        tg = lambda n: _TAG.__setitem__(0, n)
        tg('N1')
        rmsnorm(lambda kt: sl(xT[:, kt, 0:N]), N, "norm_mix", hcur, t0=14)
        if smp:
            b.dma(stg[:].rr("p (k s) -> p k s", k=8), sshift[l].rr("(k p) s -> p k s", p=128))
            for kt in range(8):
                b.cp(hx(kt)[:, :, 0:1], stg[:, kt * NSS:(kt + 1) * NSS].rr("p (s o) -> p s o", o=1), "pool")
        else:
            for kt in range(8):
                b.cp(arena[:, kt, 0:1], hlast[:, l, kt, :], "pool")
        for kt in range(8):
            b.cp(hxb[:, kt, 0:EN], arena[:, kt, 0:EN], "pool")
        if smp:
            for kt in range(8):
                b.cp(stg2[:, kt * NSS:(kt + 1) * NSS].rr("p (s o) -> p s o", o=1), hx(kt)[:, :, L:L + 1], "pool")
            b.dma(shs[l], stg2[:])
        else:
            for kt in range(8):
                b.cp(hlast[:, l, kt, :], arena[:, kt, L:L + 1], "pool")
            if last_p:
                b.dma(shp[l], hlast[:, l].rr("p k o -> p (k o)"))

        tg('L')
        if do_rwkv:
            for kt in range(8):
                b.tt(sl(ar(8 + kt)), hprev(kt), hcur(kt), ALU.subtract)
            mixes = [("mu_w", w1b, 64, 0, AF.Tanh), ("mu_a", a1b, 64, 1, None), ("mu_g", g1b, 160, 3, AF.Sigmoid)]
            if l >= 1:
                mixes.append(("mu_vres", v1b, 32, 2, None))
            for (mun, wb, R_, slot, fn) in mixes:
                for kt in range(8):
                    b.stt(sl(mx[:, kt, 0:N]), sl(ar(8 + kt)), pvc(mun, kt), hcur(kt), ALU.mult, ALU.add)
                for (m0, msz, sl_) in ([(0, R_, slot)] if R_ <= 128 else [(0, 128, slot), (128, R_ - 128, slot + 1)]):
                    ps = bank()
                    for kt in range(8):
                        b.mm(ps[0:msz, 0:N], wb[:, kt, m0:m0 + msz], mx[:, kt, 0:N], kt == 0, kt == 7)
                    if fn is None:
                        b.evac(tl[0:msz, sl_, 0:N], ps[0:msz, 0:N])
                    else:
                        b.act(tl[0:msz, sl_, 0:N], ps[0:msz, 0:N], fn)

        tg('P1')
        if smp:
            rhs_fn, NP = (lambda kt: hxb[:, kt, 0:EN]), EN
            pcur = lambda ps: ext(ps[:, 0:EN])[:, :, 1:L + 1]
        else:
            rhs_fn, NP = (lambda kt: hxb[:, kt, 1:L + 1]), N
            pcur = lambda ps: sl(ps[:, 0:N])

        def ev_in(mt, ps):
            if mt < 12:
                if smp:
                    b.evac(zrkv[:, mt, 0:EN], ps[:, 0:EN])
                else:
                    b.evac(zrkv[:, mt, 1:L + 1], ps[:, 0:N])
            elif mt < 14:
                b.evac(zlx[:, mt - 12, 0:nseq * (L + 3)].rr("p (s t) -> p s t", t=L + 3)[:, :, 3:L + 3], pcur(ps))
            elif mt < 16:
                b.evac(sl(zlg[:, mt - 14, 0:N]), pcur(ps))
            else:
                b.evac(sl(zs5[:, mt - 16, 0:N]), pcur(ps))
                b.cp(zs5b[:, mt - 16, 0:N], zs5[:, mt - 16, 0:N], "pool")
        proj(w_in[l], 8, DIN, rhs_fn, NP, ev_in)
        if not smp:
            for mt in range(12):
                b.cp(zrkv[:, mt, 0:1], zlast[:, l, mt, :], "pool")
            for mt in range(12):
                b.cp(zlast[:, l, mt, :], zrkv[:, mt, L:L + 1], "pool")

        tg('lru')
        if do_lru:
            lru_stage(blk, l)
        else:
            b.memset(ycat[:, 4:6, :], 0.0)
        tg('s5')
        if do_s5:
            s5_stage(blk, l)
        else:
            b.memset(ycat[:, 6:8, :], 0.0)
        tg('rw')
        if do_rwkv:
            for p in range(4 if RWL >= 2 else 0):
                rwkv_pair(blk, l, p)
            if RWL < 2:
                b.memset(ycat[:, 0:4, :], 0.0)
        else:
            b.memset(ycat[:, 0:4, :], 0.0)
        if "ycat%d" % l in taps and pbi == 0:
            for kt in range(8):
                b.cp(arena[:, kt, 0:N], ycat[:, kt, 0:N])
            b.tap("ycat%d" % l, arena[:, 0:8, 0:N], [128, 8, N])

        tg('wout')
        def ev_res(mt, ps):
            b.tt(xT[:, mt, 0:N], xT[:, mt, 0:N], ps[:, 0:N], ALU.add)
        proj(w_out[l], 8, D, lambda kt: ycat[:, kt, 0:N], N, ev_res)
        if "x1_%d" % l in taps and pbi == 0:
            b.tap("x1_%d" % l, xT[:, :, 0:N], [128, 8, N])
        tg('attn')
        if do_attn:
            attn_stage(blk, l)
        if "x2_%d" % l in taps and pbi == 0:
            b.tap("x2_%d" % l, xT[:, :, 0:N], [128, 8, N])
        tg('ffn')
        if do_ffn:
            rmsnorm(lambda kt: xT[:, kt, 0:N], N, "norm_ffn", lambda kt: hxb[:, kt, 0:N])

            hidv = lambda mt: hid2[:, mt // 2, (mt % 2) * 512:(mt % 2) * 512 + N]

            def ev_up(mt, ps):
                if mt < 22:
                    b.act(hidv(mt), ps[:, 0:N], AF.Silu)
                else:
                    b.tt(hidv(mt - 22), hidv(mt - 22), ps[:, 0:N], ALU.mult)
            proj(w_up[l], 8, 2 * DFF, lambda kt: hxb[:, kt, 0:N], N, ev_up)
            proj(w_down[l], 22, D, hidv, N, ev_res)

    def lru_stage(blk, l):
        nseq, L, N, pbi = blk["nseq"], blk["L"], blk["N"], blk["pbi"]
        smp = pbi is None
        sl = lambda tv: tv.rr("p (s t) -> p s t", t=L)
        ar = lambda i: arena[:, i, 0:N]
        for t in range(2):
            zxe = zlx[:, t, 0:nseq * (L + 3)].rr("p (s t) -> p s t", t=L + 3)
            if smp:
                b.dma(stg[:, 0:48], sconv[l, t * 128:(t + 1) * 128, :])
                b.cp(zxe[:, :, 0:3], stg[:, 0:48].rr("p (s t) -> p s t", t=3), "pool")
                b.dma(stg[:, 64:80], slru[l, t * 128:(t + 1) * 128, :])
                h0v = stg[:, 64:80].rr("p (s o) -> p s o", o=1)
            else:
                b.cp(zxe[:, 0, 0:3], clast[:, l, t, :], "pool")
                h0v = hlru[:, l, t, :].rr("p (s o) -> p s o", o=1)
            xc = sl(ar(0))
            b.ts(xc, zxe[:, :, 0:L], pvc("conv_w0", t), pvc("conv_b", t), ALU.mult, ALU.add)
            for j in range(1, 4):
                b.stt(xc, zxe[:, :, j:j + L], pvc("conv_w%d" % j, t), xc, ALU.mult, ALU.add)
            if smp:
                b.cp(stg2[:, 0:48].rr("p (s t) -> p s t", t=3), zxe[:, :, L:L + 3], "pool")
                b.dma(convs[l, :, t * 48:(t + 1) * 48], stg2[:, 0:48])
            else:
                b.cp(clast[:, l, t, :], zxe[:, 0, L:L + 3], "pool")
                if pbi == NPB - 1:
                    b.dma(convp[l, :, t * 3:(t + 1) * 3], clast[:, l, t, :])
            xcb = bR[:, 0, 0:N]
            b.cp(xcb, ar(0))
            ps = bank()
            b.mm(ps[:, 0:N], lab[:, t, :], xcb)
            b.act(ar(1), ps[:, 0:N], AF.Sigmoid, bias=pvc("lru_ba", t))
            ps = bank()
            b.mm(ps[:, 0:N], lib[:, t, :], xcb)
            b.act(ar(2), ps[:, 0:N], AF.Sigmoid, bias=pvc("lru_bi", t))
            b.act(ar(3), ar(1), AF.Exp, scale=pvx[:, 6 + t:7 + t])
            b.act(ar(4), ar(1), AF.Exp, scale=pvx[:, 8 + t:9 + t])
            b.ts(ar(4), ar(4), -1.0, 1.0, ALU.mult, ALU.add)
            b.act(ar(4), ar(4), AF.Sqrt)
            if (not smp) and pbi == 0:
                b.memset(arena[:, 4, 0:1], 1.0, "dve")
            b.tt(ar(4), ar(4), ar(2), ALU.mult)
            b.tt(ar(4), ar(4), ar(0), ALU.mult)
            a3 = sl(ar(3))[:, :, 0:1]
            b3 = sl(ar(4))[:, :, 0:1]
            tmp = arena[:, 7, 0:nseq].rr("p (s o) -> p s o", o=1)
            b.tt(tmp, a3, h0v, ALU.mult)
            b.tt(b3, b3, tmp, ALU.add)
            b.memset(a3, 0.0, "dve")
            b.scan(ar(5), ar(3), ar(4), 0.0)
            if smp:
                b.cp(stg2[:, 64:80].rr("p (s o) -> p s o", o=1), sl(ar(5))[:, :, L - 1:L], "pool")
                b.dma(lrus[l, :, t * NSS:(t + 1) * NSS], stg2[:, 64:80])
            else:
                b.cp(hlru[:, l, t, :], arena[:, 5, N - 1:N], "pool")
                if pbi == NPB - 1:
                    b.dma(lrup[l, t, :].rr("(p o) -> p o", o=1), hlru[:, l, t, :])
            b.act(ar(6), zlg[:, t, 0:N], AF.Gelu_apprx_tanh)
            b.tt(ycat[:, 4 + t, 0:N], ar(5), ar(6), ALU.mult)

    def s5_stage(blk, l):
        nseq, L, N, pbi = blk["nseq"], blk["L"], blk["N"], blk["pbi"]
        smp = pbi is None
        nsc, Lsc = (nseq, L) if smp else (N // STAB, STAB)
        sc_ = lambda tv: tv.rr("p (s t) -> p s t", t=Lsc)
        ar = lambda i: arena[:, i, 0:N]
        if smp:
            for ri in range(2):
                b.dma(stg[:].rr("p (c s) -> p c s", c=8) if ri == 0 else stg2[:].rr("p (c s) -> p c s", c=8),
                      ss5[ri][l].rr("p (c s) -> p c s", c=8))
            sin_ = [stg, stg2]
        for ct in range(8):
            ut, ot = ct // 4, ct // 4
            rhs = zs5b[:, ut, 0:N]
            psr = bank()
            b.mm(psr[:, 0:N], Bb[:, 0, ct, :], rhs)
            psi = bank()
            b.mm(psi[:, 0:N], Bb[:, 1, ct, :], rhs)
            tb_ = lambda k: tabs[:, ct, k:k + 1, 0:Lsc].bc([128, nsc, Lsc])
            Ec, Es, Fc, Fs = tb_(0), tb_(1), tb_(2), tb_(3)
            dre, dim_, t2 = sc_(ar(0)), sc_(ar(1)), sc_(ar(2))
            b.tt(dre, sc_(psr[:, 0:N]), Fc, ALU.mult)
            b.tt(t2, sc_(psi[:, 0:N]), Fs, ALU.mult)
            b.tt(dre, dre, t2, ALU.subtract)
            b.tt(dim_, sc_(psi[:, 0:N]), Fc, ALU.mult)
            b.tt(t2, sc_(psr[:, 0:N]), Fs, ALU.mult)
            b.tt(dim_, dim_, t2, ALU.add)
            zre, zim = sc_(ar(3)), sc_(ar(4))
            c1 = s5c[:, 3, ct:ct + 1]; s1 = s5c[:, 4, ct:ct + 1]
            cL = s5u[:, ct, 0, LSUB:LSUB + 1]; sL = s5u[:, ct, 1, LSUB:LSUB + 1]
            zi = arena[:, 7, 0:4 * NSS].rr("p (k s) -> p k s", k=4)

            def rot(o_re, o_im, x_re, x_im, cc, ss, t_a, t_b):
                b.ts(t_a, x_im, ss, None, ALU.mult)
                b.stt(o_re, x_re, cc, t_a, ALU.mult, ALU.subtract)
                b.ts(t_b, x_im, cc, None, ALU.mult)
                b.stt(o_im, x_re, ss, t_b, ALU.mult, ALU.add)
            if smp:
                xre0 = sin_[0][:, ct * NSS:(ct + 1) * NSS]; xim0 = sin_[1][:, ct * NSS:(ct + 1) * NSS]
                rot(zi[:, 0, :], zi[:, 1, :], xre0, xim0, c1, s1, zi[:, 2, :], zi[:, 3, :])
                for s in range(nsc):
                    b.scan(zre[:, s, :], rhot[:, ct, 0:Lsc], dre[:, s, :], zi[:, 0, s:s + 1])
                    b.scan(zim[:, s, :], rhot[:, ct, 0:Lsc], dim_[:, s, :], zi[:, 1, s:s + 1])
            else:
                S16 = nsc
                sm_ = arena[:, 7, 0:12 * 17].rr("p (k s) -> p k s", k=12)
                b.ts(ar(5), rmask[STAB][:, 0:N], s5c[:, 2, ct:ct + 1], None, ALU.mult)
                b.scan(ar(3), ar(5), ar(0), 0.0)
                b.scan(ar(4), ar(5), ar(1), 0.0)
                ere = zre[:, :, Lsc - 1:Lsc].rr("p s o -> p (s o)"); eim = zim[:, :, Lsc - 1:Lsc].rr("p s o -> p (s o)")
                gc_ = gtab[:, ct, 0, :]; gs_ = gtab[:, ct, 1, :]
                g_re = sm_[:, 0, 0:S16]; g_im = sm_[:, 1, 0:S16]; ta = sm_[:, 2, 0:S16]; tb2 = sm_[:, 3, 0:S16]
                b.tt(ta, gc_, ere, ALU.mult)
                b.tt(tb2, gs_, eim, ALU.mult)
                b.tt(g_re, ta, tb2, ALU.add)
                b.tt(ta, gc_, eim, ALU.mult)
                b.tt(tb2, gs_, ere, ALU.mult)
                b.tt(g_im, ta, tb2, ALU.subtract)
                qre = sm_[:, 4, :]; qim = sm_[:, 5, :]
                rot(qre[:, 0:1], qim[:, 0:1], s5st[:, l, 0, ct:ct + 1], s5st[:, l, 1, ct:ct + 1], c1, s1,
                    sm_[:, 6, 0:1], sm_[:, 7, 0:1])
                rl = sm_[:, 8, 0:S16]
                b.ts(rl, onesf[:, 0:S16], s5c[:, 10, ct:ct + 1], None, ALU.mult)
                b.scan(qre[:, 1:S16 + 1], rl, g_re, qre[:, 0:1])
                b.scan(qim[:, 1:S16 + 1], rl, g_im, qim[:, 0:1])
                zr_ = sm_[:, 9, 0:S16]; zi_ = sm_[:, 10, 0:S16]
                b.tt(ta, gc_, qre[:, 0:S16], ALU.mult)
                b.tt(tb2, gs_, qim[:, 0:S16], ALU.mult)
                b.tt(zr_, ta, tb2, ALU.subtract)
                b.tt(ta, gc_, qim[:, 0:S16], ALU.mult)
                b.tt(tb2, gs_, qre[:, 0:S16], ALU.mult)
                b.tt(zi_, ta, tb2, ALU.add)
                rp = rpow[:, ct:ct + 1, 0:Lsc].bc([128, nsc, Lsc])
                b.tt(sc_(ar(6)), rp, zr_.rr("p (s o) -> p s o", o=1).bc([128, nsc, Lsc]), ALU.mult)
                b.tt(ar(3), ar(3), ar(6), ALU.add)
                b.tt(sc_(ar(6)), rp, zi_.rr("p (s o) -> p s o", o=1).bc([128, nsc, Lsc]), ALU.mult)
                b.tt(ar(4), ar(4), ar(6), ALU.add)
            xr, xi = sc_(bR[:, 0, 0:N]), sc_(bR[:, 1, 0:N])
            t5, t6 = sc_(ar(5)), sc_(ar(6))
            b.tt(t5, zre, Ec, ALU.mult)
            b.tt(t6, zim, Es, ALU.mult)
            b.tt(xr, t5, t6, ALU.subtract)
            b.tt(t5, zre, Es, ALU.mult)
            b.tt(t6, zim, Ec, ALU.mult)
            b.tt(xi, t5, t6, ALU.add)
            eC = tabs[:, ct, 0, Lsc - 1:Lsc]; eS = tabs[:, ct, 1, Lsc - 1:Lsc]
            if smp:
                zl_re = zre[:, :, Lsc - 1:Lsc].rr("p s o -> p (s o)"); zl_im = zim[:, :, Lsc - 1:Lsc].rr("p s o -> p (s o)")
                o_re = arena[:, 8, ct * NSS:(ct + 1) * NSS]; o_im = arena[:, 9, ct * NSS:(ct + 1) * NSS]
                rot(o_re, o_im, zl_re, zl_im, eC, eS, zi[:, 2, :], zi[:, 3, :])
            else:
                rot(s5st[:, l, 0, ct:ct + 1], s5st[:, l, 1, ct:ct + 1], arena[:, 3, N - 1:N], arena[:, 4, N - 1:N], eC, eS,
                    zi[:, 2, 0:1], zi[:, 3, 0:1])
            b.mm(pb[ot][:, 0:N], Cb[:, 0, ct, :], bR[:, 0, 0:N], ct % 4 == 0, False)
            b.mm(pb[ot][:, 0:N], Cb[:, 1, ct, :], bR[:, 1, 0:N], False, ct % 4 == 3)
        if smp:
            for ri in range(2):
                b.dma(s5s[ri][l], arena[:, 8 + ri, 0:8 * NSS])
        elif pbi == NPB - 1:
            for ri in range(2):
                b.dma(s5p[ri][l], s5st[:, l, ri, :])
        for ot in range(2):
            b.stt(ar(10), zs5[:, ot, 0:N], pvc("s5_d", ot), pb[ot][:, 0:N], ALU.mult, ALU.add)
            b.act(bR[:, 2 + ot, 0:N], ar(10), AF.Gelu_apprx_tanh)

        def ev_glu(mt, ps):
            if mt < 2:
                b.act(ar(11 + mt), ps[:, 0:N], AF.Identity, bias=pvc("b_glu", mt))
            else:
                b.act(ar(13), ps[:, 0:N], AF.Sigmoid, bias=pvc("b_glu", mt))
                b.tt(ycat[:, 6 + mt - 2, 0:N], ar(11 + mt - 2), ar(13), ALU.mult)
        proj(w_glu[l], 2, 512, lambda kt: bR[:, 2 + kt, 0:N], N, ev_glu)

    def rwkv_pair(blk, l, p):
        nseq, L, N, pbi = blk["nseq"], blk["L"], blk["N"], blk["pbi"]
        smp = pbi is None
        EN = nseq * (L + 1)
        C = L if smp else 64
        nch = N // C
        nlev = int(np.log2(C))
        ext = lambda tv: tv.rr("p (s t) -> p s t", t=L + 1)
        sl = lambda tv: tv.rr("p (s t) -> p s t", t=L)
        ar = lambda i: arena[:, i, 0:N]
        pcs = slice(p * 128, (p + 1) * 128)

        def mixz(dst, mt, mun):
            ze = ext(zrkv[:, mt, 0:EN])
            b.tt(sl(ar(11)), ze[:, :, 0:L], ze[:, :, 1:L + 1], ALU.subtract)
            b.stt(sl(dst), sl(ar(11)), pvc(mun, p), ze[:, :, 1:L + 1], ALU.mult, ALU.add)
        _TAG[0] = 'rw_prep'
        mixz(ar(0), p, "mu_r")
        mixz(ar(1), 4 + p, "mu_k")
        v = vfirst[:, p, 0:N] if l == 0 else ar(2)
        mixz(v, 8 + p, "mu_v")
        ps = bank()
        b.mm(ps[:, 0:N], w2b[0:64, pcs], tl[0:64, 0, 0:N])
        b.act(ar(3), ps[:, 0:N], AF.Sigmoid, bias=pvc("w0", p))
        b.ts(ar(3), ar(3), -0.6065306597126334, None, ALU.mult)
        ps = bank()
        b.mm(ps[:, 0:N], a2b[0:64, pcs], tl[0:64, 1, 0:N])
        b.act(ar(4), ps[:, 0:N], AF.Sigmoid, bias=pvc("a0", p))
        ps = bank()
        b.mm(ps[:, 0:N], g2b[:, 0, pcs], tl[:, 3, 0:N], True, False)
        b.mm(ps[:, 0:N], g2b[:, 1, pcs], tl[:, 4, 0:N], False, True)
        b.evac(ar(14), ps[:, 0:N])
        if l >= 1:
            ps = bank()
            b.mm(ps[:, 0:N], v2b[0:32, pcs], tl[0:32, 2, 0:N])
            b.act(ar(15), ps[:, 0:N], AF.Sigmoid, bias=pvc("v0", p))
            b.tt(ar(11), vfirst[:, p, 0:N], ar(2), ALU.subtract)
            b.tt(ar(11), ar(11), ar(15), ALU.mult)
            b.tt(ar(2), ar(2), ar(11), ALU.add)
        b.ts(ar(5), ar(1), pvc("k_k", p), None, ALU.mult)
        b.act(ar(11), ar(5), AF.Square)
        ps = bank()
        b.mm(ps[:, 0:N], blk1[:], ar(11))
        b.act(ar(12), ps[:, 0:N], AF.Sqrt)
        b.ts(ar(12), ar(12), 1e-12, None, ALU.max)
        b.recip(ar(12), ar(12))
        b.tt(ar(5), ar(5), ar(12), ALU.mult)
        b.ts(ar(11), ar(4), pvc("k_a", p), pvx[:, p:p + 1], ALU.mult, ALU.add)
        b.tt(ar(6), ar(1), ar(11), ALU.mult)
        b.tt(ar(7), ar(5), ar(4), ALU.mult)
        b.stt(ar(11), ar(0), pvc("r_k", p), ar(6), ALU.mult, ALU.mult)
        ps = bank()
        b.mm(ps[:, 0:N], blk1[:], ar(11))
        b.tt(ar(15), ps[:, 0:N], v, ALU.mult)
        b.scan(ar(8), rmask[C][:, 0:N], ar(3), 0.0)
        b.act(ar(9), ar(8), AF.Exp)
        b.tt(bR[:, 1, 0:N], ar(0), ar(9), ALU.mult)
        b.act(ar(9), ar(8), AF.Exp, scale=-1.0)
        b.tt(bR[:, 2, 0:N], ar(7), ar(9), ALU.mult)
        b.tt(bR[:, 3, 0:N], ar(6), ar(9), ALU.mult)
        b.tt(ar(10), ar(8), ar(3), ALU.subtract)
        b.act(ar(10), ar(10), AF.Exp)
        b.stt(bR[:, 0, 0:N], ar(5), -1.0, ar(10), ALU.mult, ALU.mult)
        c8v = ar(8).rr("p (c k) -> p c k", k=C)
        b.tt(ar(10).rr("p (c k) -> p c k", k=C), c8v[:, :, C - 1:C].bc([128, nch, C]), c8v, ALU.subtract)
        b.act(ar(10), ar(10), AF.Exp)
        b.tt(bR[:, 4, 0:N], ar(7), ar(10), ALU.mult)
        b.tt(bR[:, 5, 0:N], ar(6), ar(10), ALU.mult)
        b.act(gC[:, 0:nch].rr("p (c o) -> p c o", o=1), c8v[:, :, C - 1:C], AF.Exp)
        b.cp(bR[:, 6, 0:N], v, "pool")
        _TAG[0] = 'rw_chunk'
        G = 2
        ngrp = nch // G
        C4 = 4 * C

        def cols_of(ch):
            return slice(ch * C, (ch + 1) * C)

        def tmv(st_, q, k, hh):
            return mx[0:C, st_ * 2 + q, k * 128 + hh * 64:k * 128 + (hh + 1) * 64]

        def Mblk(st_, q, hh, blk):
            return mx[0:C, 4 + st_ * 2 + q, (hh * 4 + blk) * C:(hh * 4 + blk + 1) * C]

        def v4(tv, t):
            return tv.rr("c (q h t) -> c q h t", q=2, h=2)

        def A_steps(g):
            st_ = g % 2
            bA, bB, bG = pb[2 * st_], pb[2 * st_ + 1], pg[st_]
            bk = [bA, bB]
            steps = []

            def s_tr():
                for q in range(G):
                    cols = cols_of(g * G + q)
                    for k, src in enumerate((6, 4, 5, 0)):
                        b.tr(ptr[0:C, (q * 4 + k) * 128:(q * 4 + k + 1) * 128], bR[:, src, cols], ident[:])
                b.cp(mx[0:C, st_ * 2:st_ * 2 + 2, :], ptr[0:C, 0:1024].rr("c (q x) -> c q x", q=2), "act")
            steps.append(s_tr)

            def s_M():
                for q in range(G):
                    cols = cols_of(g * G + q)
                    for hh in range(2):
                        hs = slice(hh * 64, hh * 64 + 64)
                        for j in range(2):
                            b.mm(bk[hh][0:C, q * C4 + j * C:q * C4 + (j + 1) * C], bR[hs, 2, cols], bR[hs, j, cols])
                            b.mm(bk[hh][0:C, q * C4 + (2 + j) * C:q * C4 + (3 + j) * C], bR[hs, 3, cols], bR[hs, j, cols])
                for hh in range(2):
                    b.tt(mx[0:C, 4 + st_ * 2:6 + st_ * 2, hh * C4:(hh + 1) * C4],
                         bk[hh][0:C, 0:2 * C4].rr("c (q x) -> c q x", q=2),
                         m4[C][:, hh:hh + 1, :, :].rr("c o k t -> c o (k t)").bc([C, 2, C4]), ALU.mult)
            steps.append(s_M)

            def s_NT():
                for q in range(G):
                    for hh in range(2):
                        b.tr(ptr[0:C, (q * 2 + hh) * 64:(q * 2 + hh) * 64 + C], Mblk(st_, q, hh, 0), ident[0:C, 0:C])
                b.cp(v4(NTG[0:C, st_, :], 64)[:, :, :, 0:C], v4(ptr[0:C, 0:256], 64)[:, :, :, 0:C], "act")
                for q in range(G):
                    for hh in range(2):
                        b.mm(bG[0:C, (q * 2 + hh) * 64:(q * 2 + hh + 1) * 64], Mblk(st_, q, hh, 2), tmv(st_, q, 0, hh))
                for q in range(G):
                    X0q = XbG[0:C, q, :].rr("c (h x) -> c h x", h=2)
                    b.cp(X0q[:, :, 64:128], bG[0:C, q * 128:(q + 1) * 128].rr("c (h i) -> c h i", h=2), "act")
                    b.cp(X0q[:, :, 0:64], mx[0:C, st_ * 2 + q, 384:512].rr("c (h j) -> c h j", h=2), "act")
            steps.append(s_NT)

            def mk_level(k):
                def s_lvl():
                    xp = k % 2
                    bXq = [bA, pg[2]]
                    bSq = [pg[st_], pg[1 - st_]]
                    for q in range(G):
                        Xk = XbG[0:C, xp * 2 + q, :].rr("c (h x) -> c h x", h=2)
                        if k == 0:
                            Pk = lambda hh, q=q: Mblk(st_, q, hh, 0)
                            PTk = lambda hh, q=q: v4(NTG[0:C, st_, :], 64)[:, q, hh, 0:C]
                        else:
                            Pv = PbG[0:C, ((k - 1) % 2) * 2 + q, :].rr("c (h e t) -> c h e t", h=2, e=2)
                            Pk = lambda hh, Pv=Pv: Pv[:, hh, 0, 0:C]
                            PTk = lambda hh, Pv=Pv: Pv[:, hh, 1, 0:C]
                        for hh in range(2):
                            o_ = bXq[q][0:C, hh * 128:(hh + 1) * 128]
                            b.mm(o_, ident[0:C, 0:C], Xk[:, hh, :], True, False)
                            b.mm(o_, Pk(hh), Xk[:, hh, :], False, True)
                        if k < nlev - 1:
                            for hh in range(2):
                                o = (hh * 2) * 64
                                b.mm(bSq[q][0:C, o:o + C], PTk(hh), Pk(hh))
                                b.mm(bSq[q][0:C, o + 64:o + 64 + C], Pk(hh), PTk(hh))
                    for q in range(G):
                        b.cp(XbG[0:C, (1 - xp) * 2 + q, :], bXq[q][0:C, 0:256], "act")
                        if k < nlev - 1:
                            Pn_ = PbG[0:C, (k % 2) * 2 + q, :].rr("c (a t) -> c a t", t=64)[:, :, 0:C]
                            b.cp(Pn_, bSq[q][0:C, 0:256].rr("c (a t) -> c a t", t=64)[:, :, 0:C])
                    if k == nlev - 1:
                        for q in range(G):
                            XA = XbG[0:C, (1 - xp) * 2 + q, :].rr("c (h x) -> c h x", h=2)
                            b.cp(Z1c[0:C, q * 128:(q + 1) * 128].rr("c (h j) -> c h j", h=2), XA[:, :, 0:64], "pool")
                            b.cp(Z2f[0:C, st_, q * 128:(q + 1) * 128].rr("c (h i) -> c h i", h=2), XA[:, :, 64:128], "pool")
                return s_lvl
            for k in range(nlev):
                steps.append(mk_level(k))

            def s_fin():
                for q in range(G):
                    b.tr(ptr[:, q * 64:q * 64 + C], Z1c[0:C, q * 128:(q + 1) * 128], ident[0:C, 0:C])
                b.cp(Z1T[:, st_, :].rr("p (q t) -> p q t", q=2)[:, :, 0:C], ptr[:, 0:128].rr("p (q t) -> p q t", q=2)[:, :, 0:C], "act")
                for q in range(G):
                    for hh in range(2):
                        hs = slice(hh * 64, hh * 64 + 64)
                        b.mm(bG[hs, q * 64:(q + 1) * 64], tmv(st_, q, 2, hh), tmv(st_, q, 0, hh))
                b.cp(KVf[:, st_, :], bG[:, 0:128])
            steps.append(s_fin)
            return steps

        def S_io(ch):
            if smp:
                return sw[:, (ch % 2), :], sw[:, 2 + (ch % 2), :]
            return ST[:, l, p, :], ST[:, l, p, :]

        def B_steps(g):
            st_ = g % 2
            bA, bB = pb[2 * st_], pb[2 * st_ + 1]
            bk = [bA, bB]
            steps = []
            for q in range(G):
                ch = g * G + q
                cols = cols_of(ch)
                s2 = ch % 2
                si, so = S_io(ch)
                UT = lambda hh, s2=s2: Xb[0:C, s2, hh * 64:(hh + 1) * 64]

                def s_u(ch=ch, q=q, cols=cols, s2=s2, si=si, so=so):
                    if smp:
                        b.dma(si, swkv[l, :, ch * 256 + p * 64: ch * 256 + (p + 1) * 64])
                    if smp or ch == 0:
                        b.cp(STb[:, s2, :], si, "act")
                    b.stt(Stmp[:, s2, :], si, gC[:, ch:ch + 1], KVf[:, st_, q * 64:(q + 1) * 64], ALU.mult, ALU.add)
                    for hh in range(2):
                        hs = slice(hh * 64, hh * 64 + 64)
                        b.mm(bk[hh][0:C, 0:64], Z1T[hs, st_, q * 64:q * 64 + C], STb[hs, s2, :])
                    for hh in range(2):
                        b.tt(Xb[0:C, s2, hh * 64:(hh + 1) * 64], bk[hh][0:C, 0:64],
                             v4(Z2f[0:C, st_, :], 64)[:, q, hh, :], ALU.add)
                steps.append(s_u)

                def s_s(ch=ch, q=q, cols=cols, s2=s2, si=si, so=so, UT=UT):
                    for hh in range(2):
                        hs = slice(hh * 64, hh * 64 + 64)
                        b.mm(bA[hs, 64:128], tmv(st_, q, 1, hh), UT(hh))
                    if not smp:
                        b.tt(STb[:, 1 - s2, :], Stmp[:, s2, :], bA[:, 64:128], ALU.add)
                    b.tt(so, Stmp[:, s2, :], bA[:, 64:128], ALU.add)
                    if smp:
                        b.dma(wkvs[l, :, ch * 256 + p * 64: ch * 256 + (p + 1) * 64], so)
                steps.append(s_s)

                def s_y(ch=ch, q=q, cols=cols, s2=s2, UT=UT):
                    b.mm(bA[0:64, 128:128 + C], STb[0:64, s2, :], bR[0:64, 1, cols])
                    b.mm(bB[64:128, 128:128 + C], STb[64:128, s2, :], bR[64:128, 1, cols])
                    for hh in range(2):
                        hs = slice(hh * 64, hh * 64 + 64)
                        b.mm(bA[hs, 192:192 + C], UT(hh), Mblk(st_, q, hh, 1))
                        b.mm(bA[hs, 256:256 + C], tmv(st_, q, 0, hh), Mblk(st_, q, hh, 3))
                    b.cp(arena[0:64, 13, cols], bA[0:64, 128:128 + C], "act")
                    b.cp(arena[64:128, 13, cols], bB[64:128, 128:128 + C], "act")
                    b.tt(arena[:, 13, cols], arena[:, 13, cols], bA[:, 192:192 + C], ALU.add)
                    b.tt(arena[:, 13, cols], arena[:, 13, cols], bA[:, 256:256 + C], ALU.add)
                steps.append(s_y)
            if len(steps) == 6:
                steps = [steps[0], steps[1], steps[3], steps[2], steps[4], steps[5]]
            return steps

        if RWL >= 3 and nch > 0:
            ASTOP = int(os.environ.get("ASTOP", "99")); BSTOP = int(os.environ.get("BSTOP", "3"))
            for f in A_steps(0)[:ASTOP]:
                f()
            for g in range(ngrp):
                As = A_steps(g + 1)[:ASTOP] if g + 1 < ngrp else []
                Bs = B_steps(g)
                for i_ in range(max(len(As), len(Bs))):
                    if i_ < len(Bs):
                        Bs[i_]()
                    if i_ < len(As):
                        As[i_]()
        if (not smp) and pbi == NPB - 1:
            b.dma(wkvp[l, :, p * 64:(p + 1) * 64], ST[:, l, p, :])
        _TAG[0] = 'rw_post'
        ps = bank()
        b.mm(ps[:, 0:N], blk1[:], ar(13))
        b.stt(ar(9), ps[:, 0:N], -1.0 / 64, ar(13), ALU.mult, ALU.add)
        b.act(ar(10), ar(9), AF.Square)
        ps = bank()
        b.mm(ps[:, 0:N], blk1[:], ar(10))
        b.act(ar(10), ps[:, 0:N], AF.Sqrt, bias=pvx[:, 5:6], scale=1.0 / 64)
        b.recip(ar(10), ar(10))
        b.tt(ar(9), ar(9), ar(10), ALU.mult)
        b.ts(ar(9), ar(9), pvc("ln_w", p), pvc("ln_b", p), ALU.mult, ALU.add)
        b.tt(ar(9), ar(9), ar(15), ALU.add)
        b.tt(ycat[:, p, 0:N], ar(9), ar(14), ALU.mult)

    def attn_stage(blk, l):
        nseq, L, N, pbi = blk["nseq"], blk["L"], blk["N"], blk["pbi"]
        smp = pbi is None
        rmsnorm(lambda kt: xT[:, kt, 0:N], N, "norm_mem_q", lambda kt: hxb[:, kt, 0:N])
        proj(wq[l], 8, D, lambda kt: hxb[:, kt, 0:N], N, lambda mt, ps: b.evac(qb[:, mt, 0:N], ps[:, 0:N]))
        SC = 1.0 / 16.0
        if not smp:
            for h in range(4):
                for tt_ in range(N // 128):
                    tcs = slice(tt_ * 128, (tt_ + 1) * 128)
                    ps = bank()
                    for dt in range(2):
                        b.mm(ps[:, 0:NMEM], qb[:, 2 * h + dt, tcs], kT[:, l, 2 * h + dt, :], dt == 0, dt == 1)
                    i = tt_ % 2
                    pe = arena[:, i, 0:NMEM]
                    b.rmax(sm[:, 0:1], ps[:, 0:NMEM])
                    b.ts(sm[:, 1:2], sm[:, 0:1], -SC, None, ALU.mult)
                    b.act(pe, ps[:, 0:NMEM], AF.Exp, bias=sm[:, 1:2], scale=SC)
                    b.rsum(sm[:, 2:3], pe)
                    b.recip(sm[:, 3:4], sm[:, 2:3])
                    b.ts(Pn[:, i, :], pe, sm[:, 3:4], None, ALU.mult)
                    for mt in range(2):
                        b.tr(ptr[:, mt * 128:(mt + 1) * 128], Pn[:, i, mt * 128:(mt + 1) * 128], ident[:])
                    b.evac(PTb[:, :, tcs], ptr[:, 0:256].rr("m (k t) -> m k t", k=2))
                for dt in range(2):
                    ps = bank()
                    for mt in range(2):
                        b.mm(ps[:, 0:N], vM[:, l, mt, h * 256 + dt * 128:h * 256 + (dt + 1) * 128], PTb[:, mt, 0:N], mt == 0, mt == 1)
                    b.evac(ycat[:, 2 * h + dt, 0:N], ps[:, 0:N])
        else:
            for s in range(nseq):
                qc = slice(s * L, (s + 1) * L)
                if s % 2 == 0:
                    ckv = lambda a, sl_: ckb[:, a, sl_]
                    cvv = lambda a, sl_: cvb[:, a, sl_]
                    b.dma(ckb[:], ck[l, s].rr("p (a x) -> p a x", a=2), q="pool")
                    b.dma(cvb[:], cv[l, s].rr("p (a x) -> p a x", a=2), q="pool")
                else:
                    ckv = lambda a, sl_: arb[:, 6 + a, sl_]
                    cvv = lambda a, sl_: arb[:, 4 + a, sl_]
                    b.dma(arb[:, 6:8, 0:1024], ck[l, s].rr("p (a x) -> p a x", a=2), q="pool")
                    b.dma(arb[:, 4:6, 0:1024], cv[l, s].rr("p (a x) -> p a x", a=2), q="pool")
                pss = [bank(), bank()]
                for h in range(4):
                    for dt in range(2):
                        b.mm(pss[h // 2][0:L, (h % 2) * 256:(h % 2 + 1) * 256], qb[:, 2 * h + dt, qc],
                             ckv((h * 2 + dt) // 4, slice(((h * 2 + dt) % 4) * 256, ((h * 2 + dt) % 4 + 1) * 256)), dt == 0, dt == 1)
                for hf in range(2):
                    b.rmax(sm[0:L, hf * 2:hf * 2 + 2], pss[hf][0:L, 0:512].rr("q (h m) -> q h m", h=2))
                b.ts(sm[0:L, 4:8], sm[0:L, 0:4], -SC, None, ALU.mult)
                for h in range(4):
                    b.act(arena[0:L, h // 2, (h % 2) * 256:(h % 2 + 1) * 256], pss[h // 2][0:L, (h % 2) * 256:(h % 2 + 1) * 256],
                          AF.Exp, bias=sm[0:L, 4 + h:5 + h], scale=SC)
                for hf in range(2):
                    peh = arena[0:L, hf, 0:512].rr("q (h m) -> q h m", h=2)
                    b.rsum(sm[0:L, hf * 2:hf * 2 + 2], peh)
                b.recip(sm[0:L, 0:4], sm[0:L, 0:4])
                for hf in range(2):
                    peh = arena[0:L, hf, 0:512].rr("q (h m) -> q h m", h=2)
                    pnh = PTb[0:L, hf, 0:512].rr("q (h m) -> q h m", h=2)
                    b.tt(pnh, peh, sm[0:L, hf * 2:hf * 2 + 2].rr("q (h o) -> q h o", o=1).bc([L, 2, NMEM]), ALU.mult)
                for h in range(4):
                    for mt in range(2):
                        b.tr(ptr[:, (h * 2 + mt) * L:(h * 2 + mt + 1) * L],
                             PTb[0:L, h // 2, (h % 2) * 256 + mt * 128:(h % 2) * 256 + (mt + 1) * 128], ident[0:L, 0:L])
                pt8 = Pn[:, 0, 0:8 * L]
                b.evac(pt8, ptr[:, 0:8 * L])
                ps = bank()
                for h in range(4):
                    for dt in range(2):
                        for mt in range(2):
                            b.mm(ps[:, (h * 2 + dt) * L:(h * 2 + dt + 1) * L],
                                 cvv(mt, slice(h * 256 + dt * 128, h * 256 + (dt + 1) * 128)),
                                 Pn[:, 0, (h * 2 + mt) * L:(h * 2 + mt + 1) * L], mt == 0, mt == 1)
                b.evac(ycat[:, :, qc], ps[:, 0:8 * L].rr("d (k q) -> d k q", q=L))

        def ev_res(mt, ps):
            b.tt(xT[:, mt, 0:N], xT[:, mt, 0:N], ps[:, 0:N], ALU.add)
        proj(wo[l], 8, D, lambda kt: ycat[:, kt, 0:N], N, ev_res)

    if do_attn and any(bi < NPB for bi in n_blocks):
        for l in range(n_layers):
            mem_phase(l)
    for bi in n_blocks:
        if bi < NPB:
            blk = {"nseq": 1, "L": TB, "N": TB, "pbi": bi}
            for kt in range(8):
                b.dma(xT[:, kt, :], xp[kt * 128:(kt + 1) * 128, bi * TB:(bi + 1) * TB])
        else:
            blk = {"nseq": NSS, "L": LS, "N": NSS * LS, "pbi": None}
            for kt in range(8):
                b.dma(xT[:, kt, 0:128], xs[kt * 128:(kt + 1) * 128, :])
        N = blk["N"]
        for l in range(n_layers):
            load_layer_consts(l)
            layer(blk, l)
        _TAG[0] = 'final'
        rmsnorm(lambda kt: xT[:, kt, 0:N], N, "norm_final", lambda kt: arena[:, kt, 0:N])
        for kt in range(8):
            if bi < NPB:
                b.dma(yp[kt * 128:(kt + 1) * 128, bi * TB:(bi + 1) * TB], arena[:, kt, 0:N])
            else:
                b.dma(ys[kt * 128:(kt + 1) * 128, :], arena[:, kt, 0:N])


def make_in_maps(inp, cores=None):
    f = lambda a: np.ascontiguousarray(np.asarray(a, np.float32))
    shared = {}
    for k_dev, k_in in (("w_in", "w_in"), ("w_out", "w_out"), ("wq", "mem_wq"), ("wk", "mem_wk"), ("wv", "mem_wv"),
                        ("wo", "mem_wo"), ("w_up", "ffn_w_up"), ("w_down", "ffn_w_down"), ("w_glu", "s5_w_glu"),
                        ("w1", "w1"), ("a1", "a1"), ("v1", "v1"), ("g1", "g1"), ("w2", "w2"), ("a2", "a2"),
                        ("v2", "v2"), ("g2", "g2")):
        shared[k_dev] = f(inp[k_in])
    shared["pvec"] = np.stack([pack_pvec(inp, l) for l in range(2)])
    BC = [s5_blocks(inp, l) for l in range(2)]
    shared["s5B"] = f(np.stack([x[0] for x in BC]).reshape(2, 2, 128, 1024))
    shared["s5C"] = f(np.stack([x[1] for x in BC]).reshape(2, 2, 128, 1024))
    shared["lruA"] = f(np.stack([lru_blocks(inp["lru_wa"][l]) for l in range(2)]).reshape(2, 128, 256))
    shared["lruI"] = f(np.stack([lru_blocks(inp["lru_wi"][l]) for l in range(2)]).reshape(2, 128, 256))
    maps = []
    for c in (range(NCORES) if cores is None else cores):
        sq = slice(c * NSS, (c + 1) * NSS)
        m = dict(shared)
        m["xp"] = f(np.asarray(inp["x_prompt"][c]).T)
        m["xs"] = f(np.asarray(inp["x_sample"][sq]).reshape(NSS * LS, D).T)
        m["mem"] = f(np.asarray(inp["mem_prompt"][c]).T)
        m["sshift"] = f(np.asarray(inp["state_shift"][:, sq]).transpose(0, 2, 1))
        w = np.asarray(inp["state_wkv"][:, sq]).reshape(2, NSS, 4, 2, 64, 64)
        m["swkv"] = f(w.transpose(0, 3, 5, 1, 2, 4).reshape(2, 128, NSS * 256))
        m["sconv"] = f(np.asarray(inp["state_conv"][:, sq]).transpose(0, 3, 1, 2).reshape(2, 256, NSS * 3))
        m["slru"] = f(np.asarray(inp["state_lru"][:, sq]).transpose(0, 2, 1))
        for nm, key in (("ss5re", "state_s5_re"), ("ss5im", "state_s5_im")):
            a = np.asarray(inp[key][:, sq]).reshape(2, NSS, 8, 2, 64)
            m[nm] = f(a.transpose(0, 3, 4, 2, 1).reshape(2, 128, 8 * NSS))
        k_ = np.asarray(inp["cache_mem_k"][:, sq]).reshape(2, NSS, NMEM, 4, 2, 128)
        m["ck"] = f(k_.transpose(0, 1, 5, 3, 4, 2).reshape(2, NSS, 128, 2048))
        v_ = np.asarray(inp["cache_mem_v"][:, sq]).reshape(2, NSS, 2, 128, D)
        m["cv"] = f(v_.transpose(0, 1, 3, 2, 4).reshape(2, NSS, 128, 2048))
        maps.append(m)
    return maps


def assemble(results):
    R = results
    cat = lambda fn: np.ascontiguousarray(np.concatenate([fn(r) for r in R], axis=0 if True else 0))
    y_prompt = np.stack([r["yp"].T for r in R])
    y_sample = np.concatenate([r["ys"].T.reshape(NSS, LS, D) for r in R], 0)
    vec = lambda a: a.transpose(0, 2, 1).reshape(2, -1)
    shift_p = np.stack([vec(r["shp"]) for r in R], 1)
    shift_s = np.concatenate([r["shs"].reshape(2, 128, 8, NSS).transpose(0, 3, 2, 1).reshape(2, NSS, D) for r in R], 1)
    wkv_p = np.stack([r["wkvp"].reshape(2, 2, 64, 4, 64).transpose(0, 3, 1, 4, 2).reshape(2, 8, 64, 64) for r in R], 1)
    wkv_s = np.concatenate([r["wkvs"].reshape(2, 2, 64, NSS, 4, 64).transpose(0, 3, 4, 1, 5, 2).reshape(2, NSS, 8, 64, 64)
                            for r in R], 1)
    conv_p = np.stack([r["convp"].reshape(2, 128, 2, 3).transpose(0, 3, 2, 1).reshape(2, 3, 256) for r in R], 1)
    conv_s = np.concatenate([r["convs"].reshape(2, 128, 2, NSS, 3).transpose(0, 3, 4, 2, 1).reshape(2, NSS, 3, 256) for r in R], 1)
    lru_p = np.stack([r["lrup"].reshape(2, 256) for r in R], 1)
    lru_s = np.concatenate([r["lrus"].reshape(2, 128, 2, NSS).transpose(0, 3, 2, 1).reshape(2, NSS, 256) for r in R], 1)

    def s5p_(k):
        return np.stack([r[k].reshape(2, 2, 64, 8).transpose(0, 3, 1, 2).reshape(2, 16, 64) for r in R], 1)

    def s5s_(k):
        return np.concatenate([r[k].reshape(2, 2, 64, 8, NSS).transpose(0, 4, 3, 1, 2).reshape(2, NSS, 16, 64) for r in R], 1)
    mem_k = np.stack([r["memk"].reshape(2, NMEM, 4, 256) for r in R], 1)
    mem_v = np.stack([r["memv"].reshape(2, NMEM, 4, 256) for r in R], 1)
    outs = (y_prompt, y_sample, shift_p, shift_s, wkv_p, wkv_s, conv_p, conv_s, lru_p, lru_s,
            s5p_("s5rep"), s5s_("s5res"), s5p_("s5imp"), s5s_("s5ims"), mem_k, mem_v)
    return tuple(np.ascontiguousarray(o, dtype=np.float32) for o in outs)


_NC_CACHE = {}


def kernel(**inputs):
    if "nc" not in _NC_CACHE:
        _NC_CACHE["nc"] = build()[0]
    nc = _NC_CACHE["nc"]
    maps = make_in_maps(inputs)
    res = run_bass_kernel_spmd(nc, maps, core_ids=list(range(NCORES)))
    return assemble(res.results)
```

```python
import os
import numpy as np
import concourse.bass as bass
import concourse.mybir as mybir
from concourse.bass_utils import run_bass_kernel_spmd
from contextlib import ExitStack

F32 = mybir.dt.float32
BF16 = mybir.dt.bfloat16
I32 = mybir.dt.int32
AF = mybir.ActivationFunctionType
ALU = mybir.AluOpType
AX = mybir.AxisListType

NCORES = 8
D = 1024
SEQ = 2048
TB = 512
NPB = SEQ // TB
NSS = 16
LS = 8
NMEM = 256
DFF = 2816
DIN = 2304
STAB = 32
LSUB = 5
RWL = int(os.environ.get('RWL', '9'))


class Res:
    __slots__ = ("lw", "rd")

    def __init__(self):
        self.lw = None
        self.rd = []


class Op:
    __slots__ = ("eng", "fn", "deps", "is_dma", "sem", "val", "needed", "slot", "tag")


_TAG = ["-"]
_NAMES = {}


class Prog:
    ENGS = ("pe", "act", "dve", "pool", "sp")

    def __init__(self, nc, n_dma_sems=16):
        self.nc = nc
        self.ops = []
        self.by_eng = {e: [] for e in self.ENGS}
        self.n_dma_sems = n_dma_sems
        self.dma_count = {e: 0 for e in self.ENGS}

    def add(self, eng, fn, reads=(), writes=(), dma=False):
        op = Op()
        op.eng = eng
        op.fn = fn
        op.is_dma = dma
        op.tag = _TAG[0]
        deps = set()
        for r in reads:
            if r.lw is not None:
                deps.add(r.lw)
        for w in writes:
            if w.lw is not None:
                deps.add(w.lw)
            for rr in w.rd:
                deps.add(rr)
        for r in reads:
            r.rd.append(op)
        for w in writes:
            w.lw = op
            w.rd = []
        deps.discard(op)
        op.deps = deps
        op.needed = False
        op.sem = None
        op.val = 0
        op.slot = 0
        if dma:
            k = self.dma_count[eng]
            self.dma_count[eng] = k + 1
            op.slot = k % self.n_dma_sems
            op.val = 16 * (k // self.n_dma_sems + 1)
        self.ops.append(op)
        self.by_eng[eng].append(op)
        return op

    def emit(self, stack):
        nc = self.nc
        engobj = {"pe": "tensor", "act": "scalar", "dve": "vector", "pool": "gpsimd", "sp": "sync"}
        for op in self.ops:
            for d in op.deps:
                if d.is_dma:
                    continue
                if d.eng == "pe" and op.eng == "pe" and not op.is_dma:
                    continue
                d.needed = True
        EPOCH = 3000
        for e in ("pe", "act", "dve", "pool"):
            sem = None
            cnt = EPOCH
            nep = 0
            for op in self.by_eng[e]:
                if not op.is_dma and op.needed:
                    if cnt >= EPOCH:
                        sem = stack.enter_context(nc.semaphore("s_%s_%d" % (e, nep)))
                        nep += 1
                        cnt = 0
                    cnt += 1
                    op.val = cnt
                    op.sem = sem
        for e in self.ENGS:
            if self.dma_count[e] > 0:
                sems = [stack.enter_context(nc.semaphore("d_%s_%d" % (e, i)))
                        for i in range(min(self.n_dma_sems, self.dma_count[e]))]
                for op in self.by_eng[e]:
                    if op.is_dma:
                        op.sem = sems[op.slot]
        final_dma = {}
        for op in self.ops:
            if op.is_dma:
                final_dma[id(op.sem)] = (op.sem, op.val)
        block = stack.enter_context(nc.Block())
        prog = self

        def make(ename):
            def body(eng):
                waited = {}

                def wait(sem, val):
                    if waited.get(id(sem), 0) < val:
                        eng.wait_ge(sem, val)
                        waited[id(sem)] = val

                for op in prog.by_eng[ename]:
                    for d in op.deps:
                        if (not d.is_dma) and d.eng == "pe" and ename == "pe" and not op.is_dma:
                            continue
                        wait(d.sem, d.val)
                    if op.is_dma:
                        if op.val > 16:
                            wait(op.sem, op.val - 16)
                        ins = op.fn(eng)
                        _NAMES[ins.ins.name] = op.tag
                        ins.then_inc(op.sem, 16)
                    else:
                        ins = op.fn(eng)
                        _NAMES[ins.ins.name] = op.tag
                        if op.needed:
                            ins.then_inc(op.sem, 1)
                if ename == "sp":
                    for sem, val in final_dma.values():
                        wait(sem, val)
            return body

        for e in self.ENGS:
            if len(self.by_eng[e]) == 0 and e != "sp":
                continue
            getattr(block, engobj[e])(make(e))


class V:
    __slots__ = ("ap", "res", "allres")

    def __init__(self, ap, res, allres=None):
        self.ap = ap
        self.res = res
        self.allres = allres

    def __getitem__(self, key):
        return V(self.ap[key], self.res, self.allres)

    def rr(self, s, **kw):
        return V(self.ap.rearrange(s, **kw), self.res, self.allres)

    def bc(self, shape):
        return V(self.ap.to_broadcast(list(shape)), self.res, self.allres)

    def bitcast(self, dt):
        return V(self.ap.bitcast(dt), self.res, self.allres)


class T:
    def __init__(self, tensor, nres=1):
        self.t = tensor
        self.res = [Res() for _ in range(nres)]

    def __getitem__(self, key):
        ap = self.t[key]
        if len(self.res) > 1 and isinstance(key, tuple) and len(key) >= 2:
            k = key[1]
            if isinstance(k, int):
                return V(ap, [self.res[k]])
            if isinstance(k, slice):
                return V(ap, self.res[k])
        return V(ap, list(self.res))


class TP(T):
    def __init__(self, tensor):
        self.t = tensor
        self.res = [Res() for _ in range(8)]

    def __getitem__(self, key):
        ap = self.t[key]
        cs = key[1]
        c0 = cs.start or 0
        c1 = 512 if cs.stop is None else cs.stop
        return V(ap, self.res[c0 // 64:(c1 - 1) // 64 + 1], self.res)


def _a(x):
    return x.ap if isinstance(x, V) else x


def _r(*xs):
    out = []
    for x in xs:
        if isinstance(x, V):
            out += x.res
    return out


PV_SPEC = [("norm_mix", 8), ("norm_mem_q", 8), ("norm_ffn", 8), ("norm_mem_kv", 8), ("norm_final", 8),
           ("mu_r", 4), ("mu_k", 4), ("mu_v", 4), ("mu_w", 8), ("mu_a", 8), ("mu_g", 8), ("mu_vres", 8),
           ("w0", 4), ("a0", 4), ("v0", 4), ("k_k", 4), ("k_a", 4), ("r_k", 4), ("ln_w", 4), ("ln_b", 4),
           ("conv_w0", 2), ("conv_w1", 2), ("conv_w2", 2), ("conv_w3", 2), ("conv_b", 2),
           ("lru_ba", 2), ("lru_bi", 2), ("lru_lambda", 2),
           ("s5_are", 8), ("s5_aim", 8), ("s5_logdt", 8), ("s5_d", 2), ("b_glu", 4)]
PV_OFF = {}
_o = 0
for _n, _k in PV_SPEC:
    PV_OFF[_n] = _o
    _o += _k
NPV = _o


def _cols(v):
    v = np.asarray(v, np.float32).reshape(-1)
    return v.reshape(-1, 128).T


def pack_pvec(inp, l):
    pv = np.zeros((128, NPV), np.float32)

    def put(name, v):
        c = _cols(v)
        pv[:, PV_OFF[name]:PV_OFF[name] + c.shape[1]] = c
    put("norm_mix", inp["norm_mix"][l])
    put("norm_mem_q", inp["norm_mem_q"][l])
    put("norm_ffn", inp["norm_ffn"][l])
    put("norm_mem_kv", inp["norm_mem_kv"][l])
    put("norm_final", inp["norm_final"])
    mu = inp["mu_rkv"][l]
    put("mu_r", mu[0:512])
    put("mu_k", mu[512:1024])
    put("mu_v", mu[1024:1536])
    put("mu_w", inp["mu_wag"][l][0])
    put("mu_a", inp["mu_wag"][l][1])
    put("mu_g", inp["mu_wag"][l][2])
    if l >= 1:
        put("mu_vres", inp["mu_v"][l - 1])
        put("v0", inp["v0"][l - 1])
    put("w0", inp["w0"][l])
    put("a0", inp["a0"][l])
    put("k_k", inp["k_k"][l])
    put("k_a", inp["k_a"][l])
    put("r_k", inp["r_k"][l])
    put("ln_w", inp["ln_x_w"][l])
    put("ln_b", inp["ln_x_b"][l])
    for j in range(4):
        put("conv_w%d" % j, inp["conv_w"][l][j])
    put("conv_b", inp["conv_b"][l])
    put("lru_ba", inp["lru_ba"][l])
    put("lru_bi", inp["lru_bi"][l])
    put("lru_lambda", inp["lru_lambda"][l])
    def chan(a):
        a = np.asarray(a, np.float32).reshape(8, 2, 64)
        return a.transpose(1, 2, 0).reshape(128, 8)
    pv[:, PV_OFF["s5_are"]:PV_OFF["s5_are"] + 8] = chan(inp["s5_a_re"][l])
    pv[:, PV_OFF["s5_aim"]:PV_OFF["s5_aim"] + 8] = chan(inp["s5_a_im"][l])
    pv[:, PV_OFF["s5_logdt"]:PV_OFF["s5_logdt"] + 8] = chan(np.repeat(np.asarray(inp["s5_log_dt"][l])[:, None], 64, 1))
    put("s5_d", inp["s5_d"][l])
    put("b_glu", inp["s5_b_glu"][l])
    return pv


def s5_blocks(inp, l):
    b_re = np.asarray(inp["s5_b_re"][l], np.float32)
    b_im = np.asarray(inp["s5_b_im"][l], np.float32)
    c_re = np.asarray(inp["s5_c_re"][l], np.float32)
    c_im = np.asarray(inp["s5_c_im"][l], np.float32)
    Bb = np.zeros((2, 128, 8, 128), np.float32)
    Cb = np.zeros((2, 128, 8, 128), np.float32)
    for g in range(16):
        ct, gg, g8 = g // 2, g % 2, g % 8
        Bb[0, g8 * 16:(g8 + 1) * 16, ct, gg * 64:(gg + 1) * 64] = b_re[g].T
        Bb[1, g8 * 16:(g8 + 1) * 16, ct, gg * 64:(gg + 1) * 64] = b_im[g].T
        Cb[0, gg * 64:(gg + 1) * 64, ct, g8 * 16:(g8 + 1) * 16] = c_re[g].T
        Cb[1, gg * 64:(gg + 1) * 64, ct, g8 * 16:(g8 + 1) * 16] = c_im[g].T
    return Bb, Cb


def lru_blocks(w):
    w = np.asarray(w, np.float32)
    o = np.zeros((128, 2, 128), np.float32)
    for h in range(4):
        t, hh = h // 2, h % 2
        o[hh * 64:(hh + 1) * 64, t, hh * 64:(hh + 1) * 64] = w[h]
    return o


class Bld:
    def __init__(self, nc, st):
        self.nc = nc
        self.st = st
        self.P = Prog(nc)
        self.taps = {}
        self.flip = 0

    def sb(self, name, shape, dt, nres=1):
        return T(self.st.enter_context(self.nc.sbuf_tensor(name, list(shape), dt)), nres)

    def psum(self, name, shape, dt):
        return T(self.st.enter_context(self.nc.psum_tensor(name, list(shape), dt)), 1)

    def din(self, name, shape, dt=F32):
        return T(self.nc.dram_tensor(name, list(shape), dt, kind="ExternalInput").ap(), 1)

    def dout(self, name, shape, dt=F32):
        return T(self.nc.dram_tensor(name, list(shape), dt, kind="ExternalOutput").ap(), 1)

    def tt(self, out, a, b, op, eng="dve"):
        self.P.add(eng, lambda e: e.tensor_tensor(out=_a(out), in0=_a(a), in1=_a(b), op=op), _r(a, b), _r(out))

    def ts(self, out, a, s1, s2, op0, op1=None, eng="dve"):
        if op1 is None:
            self.P.add(eng, lambda e: e.tensor_scalar(out=_a(out), in0=_a(a), scalar1=_a(s1), scalar2=None, op0=op0),
                       _r(a, s1), _r(out))
        else:
            self.P.add(eng, lambda e: e.tensor_scalar(out=_a(out), in0=_a(a), scalar1=_a(s1), scalar2=_a(s2), op0=op0, op1=op1),
                       _r(a, s1, s2), _r(out))

    def stt(self, out, a, s, b, op0, op1):
        self.P.add("dve", lambda e: e.scalar_tensor_tensor(out=_a(out), in0=_a(a), scalar=_a(s), in1=_a(b), op0=op0, op1=op1),
                   _r(a, s, b), _r(out))

    def act(self, out, in_, func, bias=None, scale=None):
        kw = {}
        if bias is not None:
            kw["bias"] = _a(bias)
        if scale is not None:
            kw["scale"] = _a(scale)
        self.P.add("act", lambda e: e.activation(out=_a(out), in_=_a(in_), func=func, **kw), _r(in_, bias, scale), _r(out))

    def cp(self, out, in_, eng="dve"):
        if eng == "act":
            self.P.add("act", lambda e: e.copy(out=_a(out), in_=_a(in_)), _r(in_), _r(out))
        else:
            self.P.add(eng, lambda e: e.tensor_copy(out=_a(out), in_=_a(in_)), _r(in_), _r(out))

    def evac(self, out, in_):
        self.flip ^= 1
        self.cp(out, in_, "act" if self.flip else "dve")

    def mm(self, out, lhsT, rhs, start=True, stop=True):
        ob = _a(out).base_partition()
        lb = _a(lhsT).base_partition()
        kw = {}
        if ob != lb:
            kw["tile_position"] = (lb, ob)
        wr = out.allres if out.allres is not None else out.res
        self.P.add("pe", lambda e: e.matmul(_a(out), lhsT=_a(lhsT), rhs=_a(rhs), start=start, stop=stop, **kw),
                   _r(lhsT, rhs), wr)

    def tr(self, out, in_, ident):
        self.P.add("pe", lambda e: e.transpose(_a(out), _a(in_), _a(ident)), _r(in_, ident), _r(out))

    def scan(self, out, d0, d1, init):
        self.P.add("dve", lambda e: e.tensor_tensor_scan(out=_a(out), data0=_a(d0), data1=_a(d1), initial=_a(init),
                                                         op0=ALU.mult, op1=ALU.add), _r(d0, d1, init), _r(out))

    def recip(self, out, in_):
        self.P.add("dve", lambda e: e.reciprocal(out=_a(out), in_=_a(in_)), _r(in_), _r(out))

    def memset(self, v, val, eng="pool"):
        self.P.add(eng, lambda e: e.memset(_a(v), val), (), _r(v))

    def asel(self, v, pattern, op, cm, base=0):
        self.P.add("pool", lambda e: e.affine_select(out=_a(v), in_=_a(v), pattern=pattern, compare_op=op, fill=0.0,
                                                     base=base, channel_multiplier=cm), _r(v), _r(v))

    def rmax(self, out, in_):
        self.P.add("dve", lambda e: e.reduce_max(out=_a(out), in_=_a(in_), axis=AX.X), _r(in_), _r(out))

    def rsum(self, out, in_):
        self.P.add("dve", lambda e: e.reduce_sum(out=_a(out), in_=_a(in_), axis=AX.X), _r(in_), _r(out))

    def dma(self, out, in_, q="sp"):
        self.P.add(q, lambda e: e.dma_start(out=_a(out), in_=_a(in_)), _r(in_), _r(out), dma=True)

    def tap(self, name, v, shape, dt=F32):
        d = self.dout("dbg_" + name, shape, dt)
        self.dma(d[:], v)
        self.taps[name] = shape


def build(n_blocks=(0, 1, 2, 3, 4), n_layers=2, taps=(), do_rwkv=True, do_lru=True, do_s5=True, do_attn=True, do_ffn=True):
    nc = bass.Bass("TRN2", target_bir_lowering=False)
    st = ExitStack()
    with st:
        b = Bld(nc, st)
        _build(b, n_blocks, n_layers, set(taps), do_rwkv, do_lru, do_s5, do_attn, do_ffn)
        b.P.emit(st)
    return nc, b.taps


def _build(b, n_blocks, n_layers, taps, do_rwkv, do_lru, do_s5, do_attn, do_ffn):
    E = TB + 1
    xp = b.din("xp", [D, SEQ]); xs = b.din("xs", [D, 128]); mem = b.din("mem", [D, NMEM])
    sshift = b.din("sshift", [2, D, NSS]); swkv = b.din("swkv", [2, 128, NSS * 256])
    sconv = b.din("sconv", [2, 256, NSS * 3]); slru = b.din("slru", [2, 256, NSS])
    ss5 = [b.din("ss5re", [2, 128, 8 * NSS]), b.din("ss5im", [2, 128, 8 * NSS])]
    ck = b.din("ck", [2, NSS, 128, 2048]); cv = b.din("cv", [2, NSS, 128, 2048])
    w_in = b.din("w_in", [2, D, DIN]); w_out = b.din("w_out", [2, D, D])
    wq = b.din("wq", [2, D, D]); wk = b.din("wk", [2, D, D]); wv = b.din("wv", [2, D, D]); wo = b.din("wo", [2, D, D])
    w_up = b.din("w_up", [2, D, 2 * DFF]); w_down = b.din("w_down", [2, DFF, D]); w_glu = b.din("w_glu", [2, 256, 512])
    w1 = b.din("w1", [2, D, 64]); a1 = b.din("a1", [2, D, 64]); v1 = b.din("v1", [1, D, 32]); g1 = b.din("g1", [2, D, 160])
    w2 = b.din("w2", [2, 64, 512]); a2 = b.din("a2", [2, 64, 512]); v2 = b.din("v2", [1, 32, 512]); g2 = b.din("g2", [2, 160, 512])
    pvd = b.din("pvec", [2, 128, NPV]); s5B = b.din("s5B", [2, 2, 128, 1024]); s5C = b.din("s5C", [2, 2, 128, 1024])
    lruA = b.din("lruA", [2, 128, 256]); lruI = b.din("lruI", [2, 128, 256])
    yp = b.dout("yp", [D, SEQ]); ys = b.dout("ys", [D, 128])
    shp = b.dout("shp", [2, 128, 8]); shs = b.dout("shs", [2, 128, 8 * NSS])
    wkvp = b.dout("wkvp", [2, 128, 256]); wkvs = b.dout("wkvs", [2, 128, NSS * 256])
    convp = b.dout("convp", [2, 128, 6]); convs = b.dout("convs", [2, 128, 2 * NSS * 3])
    lrup = b.dout("lrup", [2, 2, 128]); lrus = b.dout("lrus", [2, 128, 2 * NSS])
    s5p = [b.dout("s5rep", [2, 128, 8]), b.dout("s5imp", [2, 128, 8])]
    s5s = [b.dout("s5res", [2, 128, 8 * NSS]), b.dout("s5ims", [2, 128, 8 * NSS])]
    memk = b.dout("memk", [2, NMEM, D]); memv = b.dout("memv", [2, NMEM, D])

    ident = b.sb("ident", [128, 128], BF16)
    onesf = b.sb("onesf", [128, 128], F32); blk1 = b.sb("blk1", [128, 128], F32)
    m4 = {64: b.sb("m4_64", [64, 2, 4, 64], BF16), 8: b.sb("m4_8", [8, 2, 4, 8], BF16)}
    mT = {64: b.sb("mT_64", [64, 2, 64], BF16), 8: b.sb("mT_8", [8, 2, 8], BF16)}
    rmask = {64: b.sb("rmask64", [128, TB], BF16), 8: b.sb("rmask8", [128, 128], BF16), STAB: b.sb("rmaskS", [128, TB], BF16)}
    pv = b.sb("pv", [128, NPV], F32); pvx = b.sb("pvx", [128, 16], F32)
    w1b = b.sb("w1b", [128, 8, 64], BF16); a1b = b.sb("a1b", [128, 8, 64], BF16)
    v1b = b.sb("v1b", [128, 8, 32], BF16); g1b = b.sb("g1b", [128, 8, 160], BF16)
    w2b = b.sb("w2b", [64, 512], BF16); a2b = b.sb("a2b", [64, 512], BF16); v2b = b.sb("v2b", [32, 512], BF16)
    g2b = b.sb("g2b", [128, 2, 512], BF16)
    Bb = b.sb("Bb", [128, 2, 8, 128], BF16); Cb = b.sb("Cb", [128, 2, 8, 128], BF16)
    lab = b.sb("lab", [128, 2, 128], BF16); lib = b.sb("lib", [128, 2, 128], BF16)
    tabs = b.sb("tabs", [128, 8, 4, STAB], F32); s5c = b.sb("s5c", [128, 12, 8], F32); s5t = b.sb("s5t", [128, 8, 8], F32)
    s5u = b.sb("s5u", [128, 8, 2, 10], F32); rhot = b.sb("rhot", [128, 8, STAB], F32)
    rpow = b.sb("rpow", [128, 8, STAB], F32); gtab = b.sb("gtab", [128, 8, 2, 16], F32); iot = b.sb("iot", [128, STAB], F32)
    hlast = b.sb("hlast", [128, 2, 8, 1], F32); zlast = b.sb("zlast", [128, 2, 12, 1], F32)
    clast = b.sb("clast", [128, 2, 2, 3], F32); hlru = b.sb("hlru", [128, 2, 2, 1], F32)
    s5st = b.sb("s5st", [128, 2, 2, 8], F32); ST = b.sb("ST", [128, 2, 4, 64], F32)
    kT = b.sb("kT", [128, 2, 8, NMEM], BF16); vM = b.sb("vM", [128, 2, 2, D], BF16)
    xT = b.sb("xT", [128, 8, TB], F32, 8); vfirst = b.sb("vfirst", [128, 4, TB], BF16, 4)
    arena = b.sb("arena", [128, 16, E], F32, 16)
    hxb = b.sb("hxb", [128, 8, E], BF16, 8); mx = b.sb("mx", [128, 8, TB], BF16, 8)
    zrkv = b.sb("zrkv", [128, 12, E], F32, 12)
    zlx = b.sb("zlx", [128, 2, TB + 3 * NSS], F32, 2); zlg = b.sb("zlg", [128, 2, TB], F32, 2)
    zs5 = b.sb("zs5", [128, 2, TB], F32, 2); zs5b = b.sb("zs5b", [128, 2, TB], BF16, 2)
    tl = b.sb("tl", [128, 5, TB], BF16, 5); bR = b.sb("bR", [128, 8, TB], BF16, 8)
    NTG = b.sb("NTG", [64, 2, 256], BF16, 2); PbG = b.sb("PbG", [64, 4, 256], BF16, 4); IPG = b.sb("IPG", [64, 2, 256], BF16, 2)
    XbG = b.sb("XbG", [64, 4, 256], BF16, 4); Z2f = b.sb("Z2f", [64, 2, 256], F32, 2); Z1c = b.sb("Z1c", [64, 256], BF16)
    Z1T = b.sb("Z1T", [128, 2, 128], BF16, 2); KVf = b.sb("KVf", [128, 2, 128], F32, 2)
    Stmp = b.sb("Stmp", [128, 2, 64], F32, 2); Xb = b.sb("Xb", [64, 2, 128], BF16, 2); STb = b.sb("STb", [128, 2, 64], BF16, 2)
    gC = b.sb("gC", [128, 16], F32); sw = b.sb("sw", [128, 4, 64], F32, 4)
    ycat = b.sb("ycat", [128, 8, TB], BF16, 8); qb = mx
    PTb = b.sb("PTb", [128, 2, TB], BF16, 2); Pn = b.sb("Pn", [128, 2, NMEM], BF16, 2); sm = b.sb("sm", [128, 8], F32)
    hid2 = T(zrkv.t.bitcast(BF16), 1); hid2.res = zrkv.res
    wp = b.sb("wp", [128, 2, 2816], BF16, 2)
    ckb = T(zlg.t.bitcast(BF16), 1); ckb.res = zlg.res; cvb = T(zs5.t.bitcast(BF16), 1); cvb.res = zs5.res
    stg = b.sb("stg", [128, 8 * NSS], F32); stg2 = b.sb("stg2", [128, 8 * NSS], F32)
    pb = [TP(b.psum("pb%d" % i, [128, 512], F32).t) for i in range(4)]
    ptr = b.psum("ptr", [128, 1024], BF16)
    pg = [b.psum("pg%d" % i, [128, 512], F32) for i in range(3)]
    gi = [0]

    def bank():
        gi[0] = (gi[0] + 1) % 3
        return pg[gi[0]]

    def pvc(name, j, n=1):
        o = PV_OFF[name] + j
        return pv[:, o:o + n]

    b.memset(onesf[:], 1.0)
    b.memset(blk1[:], 0.0)
    b.memset(blk1[0:64, 0:64], 1.0)
    b.memset(blk1[64:128, 64:128], 1.0)
    identf = arena[:, 0, 0:128]
    b.memset(identf, 1.0)
    b.asel(identf, [[-1, 128]], ALU.is_equal, 1)
    b.cp(ident[:], identf)
    for C in (64, 8):
        b.memset(m4[C][:], 1.0)
        b.memset(mT[C][:], 1.0)
        for hh in range(2):
            for blk in range(4):
                b.asel(m4[C][:, hh, blk, :], [[1, C]], ALU.is_gt if blk in (0, 2) else ALU.is_ge, -1)
            b.asel(mT[C][:, hh, :], [[-1, C]], ALU.is_gt, 1)
        b.memset(rmask[C][:], 1.0)
        b.memset(rmask[C][:].rr("p (c k) -> p c k", k=C)[:, :, 0:1], 0.0)
    b.memset(rmask[STAB][:], 1.0)
    b.memset(rmask[STAB][:].rr("p (c k) -> p c k", k=STAB)[:, :, 0:1], 0.0)
    b.scan(iot[:], onesf[:, 0:STAB], onesf[:, 0:STAB], 0.0)
    for t_ in (hlast, zlast, clast, hlru, s5st, ST):
        b.memset(t_[:], 0.0)
    b.memset(tl[:, 4, :], 0.0)
    b.memset(g2b[:, 1, :], 0.0)

    wpi = [0]

    scr = {}
    first_pass = [True]

    def proj(wv_, nk, M, rhs_fn, N, evac_fn, key=None):
        pc = min(M, 512, (2816 // nk) // 128 * 128)
        wr = wv_.rr("(k p) m -> p k m", p=128)
        if key is not None and key not in scr:
            scr[key] = T(b.nc.dram_tensor("scr_" + key, [128, nk * M], BF16, kind="Internal").ap(), 1)
        for c0 in range(0, M, pc):
            i = wpi[0] = wpi[0] ^ 1
            pcc = min(pc, M - c0)
            if key is None or first_pass[0]:
                b.dma(wp[:, i, 0:nk * pcc].rr("p (k m) -> p k m", k=nk), wr[:, :, c0:c0 + pcc], q="pool")
                if key is not None:
                    b.dma(scr[key][:, nk * c0:nk * (c0 + pcc)], wp[:, i, 0:nk * pcc])
            else:
                b.dma(wp[:, i, 0:nk * pcc], scr[key][:, nk * c0:nk * (c0 + pcc)])
            for m0 in range(0, pcc, 128):
                ps = bank()
                for kt in range(nk):
                    b.mm(ps[:, 0:N], wp[:, i, kt * pcc + m0:kt * pcc + m0 + 128], rhs_fn(kt), kt == 0, kt == nk - 1)
                evac_fn((c0 + m0) // 128, ps)

    def rmsnorm(x_fn, N, gname, out_fn, t0=14):
        ps = bank()
        for kt in range(8):
            sq = arena[:, t0 + (kt % 2), 0:N]
            b.act(sq, x_fn(kt), AF.Square)
            b.mm(ps[:, 0:N], onesf[:], sq, kt == 0, kt == 7)
        rs = arena[:, t0, 0:N]
        b.act(rs, ps[:, 0:N], AF.Sqrt, bias=pvx[:, 4:5], scale=1.0 / D)
        b.recip(rs, rs)
        for kt in range(8):
            b.stt(out_fn(kt), x_fn(kt), pvc(gname, kt), rs, ALU.mult, ALU.mult)

    def mem_phase(l):
        _TAG[0] = 'mem'
        b.dma(pv[:], pvd[l])
        b.memset(pvx[:, 4:5], 1e-6)
        for kt in range(8):
            b.dma(arena[:, kt, 0:NMEM], mem[kt * 128:(kt + 1) * 128, :])
        rmsnorm(lambda kt: arena[:, kt, 0:NMEM], NMEM, "norm_mem_kv", lambda kt: hxb[:, kt, 0:NMEM])
        proj(wk[l], 8, D, lambda kt: hxb[:, kt, 0:NMEM], NMEM, lambda mt, ps: b.evac(kT[:, l, mt, :], ps[:, 0:NMEM]))
        for (wsrc, dst, keep) in ((wk, memk, False), (wv, memv, True)):
            wr = wsrc[l].rr("(k p) m -> p k m", p=128)
            for c0 in range(0, D, 256):
                i = wpi[0] = wpi[0] ^ 1
                b.dma(wp[:, i, 0:2048].rr("p (k m) -> p k m", k=8), wr[:, :, c0:c0 + 256], q="pool")
                for mt in range(2):
                    ps = bank()
                    for kt in range(8):
                        b.mm(ps[:, 0:256], hxb[:, kt, mt * 128:(mt + 1) * 128], wp[:, i, kt * 256:(kt + 1) * 256], kt == 0, kt == 7)
                    o32 = arena[:, 8 + mt, 0:256]
                    b.evac(o32, ps[:, 0:256])
                    b.dma(dst[l, mt * 128:(mt + 1) * 128, c0:c0 + 256], o32)
                    if keep:
                        b.cp(vM[:, l, mt, c0:c0 + 256], o32, "pool")

    def load_layer_consts(l):
        _TAG[0] = 'consts'
        b.dma(pv[:], pvd[l])
        b.dma(w1b[:], w1[l].rr("(k p) m -> p k m", p=128), q="pool")
        b.dma(a1b[:], a1[l].rr("(k p) m -> p k m", p=128), q="pool")
        b.dma(g1b[:], g1[l].rr("(k p) m -> p k m", p=128), q="pool")
        b.dma(w2b[:], w2[l], q="pool")
        b.dma(a2b[:], a2[l], q="pool")
        b.dma(g2b[:, 0, :], g2[l, 0:128, :], q="pool")
        b.dma(g2b[0:32, 1, :], g2[l, 128:160, :], q="pool")
        if l >= 1:
            b.dma(v1b[:], v1[l - 1].rr("(k p) m -> p k m", p=128), q="pool")
            b.dma(v2b[:], v2[l - 1], q="pool")
        b.dma(Bb[:].rr("p r c m -> p r (c m)"), s5B[l].rr("r p x -> p r x"), q="pool")
        b.dma(Cb[:].rr("p r c m -> p r (c m)"), s5C[l].rr("r p x -> p r x"), q="pool")
        b.ts(Cb[:, 1], Cb[:, 1], -1.0, None, ALU.mult, eng="pool")
        b.dma(lab[:].rr("p t m -> p (t m)"), lruA[l], q="pool")
        b.dma(lib[:].rr("p t m -> p (t m)"), lruI[l], q="pool")
        b.memset(pvx[:, 4:5], 1e-6)
        b.memset(pvx[:, 5:6], 64e-5)
        b.ts(pvx[:, 0:4], pvc("k_a", 0, 4), -1.0, 1.0, ALU.mult, ALU.add)
        b.act(pvx[:, 6:8], pvc("lru_lambda", 0, 2), AF.Sigmoid)
        b.act(pvx[:, 6:8], pvx[:, 6:8], AF.Ln)
        b.ts(pvx[:, 8:10], pvx[:, 6:8], 16.0, None, ALU.mult)
        b.ts(pvx[:, 6:8], pvx[:, 6:8], 8.0, None, ALU.mult)
        if do_s5:
            s5_tables()

    def s5_sincos(dst_s, theta, shift):
        u = s5t[:, 0, :]; ui = s5t[:, 1, :].bitcast(I32); uf = s5t[:, 2, :]; ng = s5t[:, 3, :]
        b.ts(u, theta, 1.0 / (2 * np.pi), shift, ALU.mult, ALU.add)
        b.cp(ui, u)
        b.cp(uf, ui)
        b.tt(u, u, uf, ALU.subtract)
        b.ts(ng, u, 0.0, None, ALU.is_lt)
        b.tt(u, u, ng, ALU.add)
        b.ts(u, u, 2 * np.pi, -np.pi, ALU.mult, ALU.add)
        b.ts(u, u, -3.1415925, 3.1415925, ALU.max, ALU.min)
        b.act(dst_s, u, AF.Sin)

    def s5_tables():
        c = lambda i: s5c[:, i, :]
        b.act(c(0), pvc("s5_logdt", 0, 8), AF.Exp)
        b.tt(c(1), pvc("s5_aim", 0, 8), c(0), ALU.mult)
        b.tt(c(2), pvc("s5_are", 0, 8), c(0), ALU.mult)
        b.act(c(2), c(2), AF.Exp)
        s5_sincos(c(4), c(1), 0.5)
        s5_sincos(c(3), c(1), 0.75)
        b.tt(c(7), c(2), c(3), ALU.mult)
        b.ts(c(7), c(7), -1.0, None, ALU.add)
        b.tt(c(8), c(2), c(4), ALU.mult)
        are = pvc("s5_are", 0, 8); aim = pvc("s5_aim", 0, 8)
        b.tt(c(5), c(7), are, ALU.mult)
        b.tt(c(11), c(8), aim, ALU.mult)
        b.tt(c(5), c(5), c(11), ALU.add)
        b.tt(c(6), c(8), are, ALU.mult)
        b.tt(c(11), c(7), aim, ALU.mult)
        b.tt(c(6), c(6), c(11), ALU.subtract)
        b.tt(c(7), are, are, ALU.mult)
        b.tt(c(8), aim, aim, ALU.mult)
        b.tt(c(7), c(7), c(8), ALU.add)
        b.recip(c(7), c(7))
        b.tt(c(5), c(5), c(7), ALU.mult)
        b.tt(c(6), c(6), c(7), ALU.mult)
        Ec = tabs[:, :, 0, :]; Es = tabs[:, :, 1, :]
        b.memset(tabs[:, :, 0, 0:1], 1.0)
        b.memset(tabs[:, :, 1, 0:1], 0.0)
        b.cp(s5u[:, :, 0, 0:1], c(3).rr("p (c o) -> p c o", o=1))
        b.cp(s5u[:, :, 1, 0:1], c(4).rr("p (c o) -> p c o", o=1))
        nlev = int(np.log2(STAB))
        for k in range(9):
            uc = s5u[:, :, 0, k:k + 1]; us = s5u[:, :, 1, k:k + 1]
            if k < nlev:
                n = 1 << k
                ucb = uc.bc([128, 8, n]); usb = us.bc([128, 8, n])
                s5tmp = arena[:, 12, 0:512].rr("p (a c t) -> p a c t", a=4, c=8)
                t1 = s5tmp[:, 0, :, 0:n]; t2 = s5tmp[:, 1, :, 0:n]; t3 = s5tmp[:, 2, :, 0:n]; t4 = s5tmp[:, 3, :, 0:n]
                b.tt(t1, Ec[:, :, 0:n], ucb, ALU.mult)
                b.tt(t2, Es[:, :, 0:n], usb, ALU.mult)
                b.tt(t3, Ec[:, :, 0:n], usb, ALU.mult)
                b.tt(t4, Es[:, :, 0:n], ucb, ALU.mult)
                b.tt(Ec[:, :, n:2 * n], t1, t2, ALU.subtract)
                b.tt(Es[:, :, n:2 * n], t3, t4, ALU.add)
            if k < 9:
                a = s5t[:, 4, :].rr("p (c o) -> p c o", o=1); bb_ = s5t[:, 5, :].rr("p (c o) -> p c o", o=1)
                b.tt(a, uc, uc, ALU.mult)
                b.tt(bb_, us, us, ALU.mult)
                b.tt(s5u[:, :, 0, k + 1:k + 2], a, bb_, ALU.subtract)
                b.tt(a, uc, us, ALU.mult)
                b.ts(s5u[:, :, 1, k + 1:k + 2], a, 2.0, None, ALU.mult)
        fre = c(5).rr("p (c o) -> p c o", o=1).bc([128, 8, STAB]); fim = c(6).rr("p (c o) -> p c o", o=1).bc([128, 8, STAB])
        Fc = tabs[:, :, 2, :]; Fs = tabs[:, :, 3, :]
        t1 = arena[:, 15, 0:8 * STAB].rr("p (c t) -> p c t", t=STAB)
        b.tt(Fc, Ec, fre, ALU.mult)
        b.tt(t1, Es, fim, ALU.mult)
        b.tt(Fc, Fc, t1, ALU.add)
        b.tt(Fs, Ec, fim, ALU.mult)
        b.tt(t1, Es, fre, ALU.mult)
        b.tt(Fs, Fs, t1, ALU.subtract)
        b.cp(rhot[:], c(2).rr("p (c o) -> p c o", o=1).bc([128, 8, STAB]))
        b.tt(c(9), pvc("s5_are", 0, 8), c(0), ALU.mult)
        b.act(c(10), c(9), AF.Exp, scale=float(STAB))
        for ct in range(8):
            b.act(rpow[:, ct, :], iot[:], AF.Exp, scale=s5c[:, 9, ct:ct + 1])
        Gc = gtab[:, :, 0, :]; Gs = gtab[:, :, 1, :]
        b.memset(gtab[:, :, 0, 0:1], 1.0)
        b.memset(gtab[:, :, 1, 0:1], 0.0)
        for k in range(4):
            n = 1 << k
            uc = s5u[:, :, 0, LSUB + k:LSUB + k + 1].bc([128, 8, n]); us = s5u[:, :, 1, LSUB + k:LSUB + k + 1].bc([128, 8, n])
            s5tmp = arena[:, 12, 0:512].rr("p (a c t) -> p a c t", a=4, c=8)
            t1 = s5tmp[:, 0, :, 0:n]; t2 = s5tmp[:, 1, :, 0:n]; t3 = s5tmp[:, 2, :, 0:n]; t4 = s5tmp[:, 3, :, 0:n]
            b.tt(t1, Gc[:, :, 0:n], uc, ALU.mult)
            b.tt(t2, Gs[:, :, 0:n], us, ALU.mult)
            b.tt(t3, Gc[:, :, 0:n], us, ALU.mult)
            b.tt(t4, Gs[:, :, 0:n], uc, ALU.mult)
            b.tt(Gc[:, :, n:2 * n], t1, t2, ALU.subtract)
            b.tt(Gs[:, :, n:2 * n], t3, t4, ALU.add)

    def layer(blk, l):
        nseq, L, N, pbi = blk["nseq"], blk["L"], blk["N"], blk["pbi"]
        smp = pbi is None
        EN = nseq * (L + 1)
        last_p = (not smp) and pbi == NPB - 1

        def ext(tv):
            return tv.rr("p (s t) -> p s t", t=L + 1)

        def sl(tv):
            return tv.rr("p (s t) -> p s t", t=L)
        hx = lambda kt: ext(arena[:, kt, 0:EN])
        hcur = lambda kt: hx(kt)[:, :, 1:L + 1]
        hprev = lambda kt: hx(kt)[:, :, 0:L]
        ar = lambda i: arena[:, i, 0:N]

        tg = lambda n: _TAG.__setitem__(0, n)
        tg('N1')
        rmsnorm(lambda kt: sl(xT[:, kt, 0:N]), N, "norm_mix", hcur, t0=14)
        if smp:
            b.dma(stg[:].rr("p (k s) -> p k s", k=8), sshift[l].rr("(k p) s -> p k s", p=128))
            for kt in range(8):
                b.cp(hx(kt)[:, :, 0:1], stg[:, kt * NSS:(kt + 1) * NSS].rr("p (s o) -> p s o", o=1), "pool")
        else:
            for kt in range(8):
                b.cp(arena[:, kt, 0:1], hlast[:, l, kt, :], "pool")
        for kt in range(8):
            b.cp(hxb[:, kt, 0:EN], arena[:, kt, 0:EN], "pool")
        if smp:
            for kt in range(8):
                b.cp(stg2[:, kt * NSS:(kt + 1) * NSS].rr("p (s o) -> p s o", o=1), hx(kt)[:, :, L:L + 1], "pool")
            b.dma(shs[l], stg2[:])
        else:
            for kt in range(8):
                b.cp(hlast[:, l, kt, :], arena[:, kt, L:L + 1], "pool")
            if last_p:
                b.dma(shp[l], hlast[:, l].rr("p k o -> p (k o)"))

        tg('L')
        if do_rwkv:
            for kt in range(8):
                b.tt(sl(ar(8 + kt)), hprev(kt), hcur(kt), ALU.subtract)
            mixes = [("mu_w", w1b, 64, 0, AF.Tanh), ("mu_a", a1b, 64, 1, None), ("mu_g", g1b, 160, 3, AF.Sigmoid)]
            if l >= 1:
                mixes.append(("mu_vres", v1b, 32, 2, None))
            for (mun, wb, R_, slot, fn) in mixes:
                for kt in range(8):
                    b.stt(sl(mx[:, kt, 0:N]), sl(ar(8 + kt)), pvc(mun, kt), hcur(kt), ALU.mult, ALU.add)
                for (m0, msz, sl_) in ([(0, R_, slot)] if R_ <= 128 else [(0, 128, slot), (128, R_ - 128, slot + 1)]):
                    ps = bank()
                    for kt in range(8):
                        b.mm(ps[0:msz, 0:N], wb[:, kt, m0:m0 + msz], mx[:, kt, 0:N], kt == 0, kt == 7)
                    if fn is None:
                        b.evac(tl[0:msz, sl_, 0:N], ps[0:msz, 0:N])
                    else:
                        b.act(tl[0:msz, sl_, 0:N], ps[0:msz, 0:N], fn)

        tg('P1')
        if smp:
            rhs_fn, NP = (lambda kt: hxb[:, kt, 0:EN]), EN
            pcur = lambda ps: ext(ps[:, 0:EN])[:, :, 1:L + 1]
        else:
            rhs_fn, NP = (lambda kt: hxb[:, kt, 1:L + 1]), N
            pcur = lambda ps: sl(ps[:, 0:N])

        def ev_in(mt, ps):
            if mt < 12:
                if smp:
                    b.evac(zrkv[:, mt, 0:EN], ps[:, 0:EN])
                else:
                    b.evac(zrkv[:, mt, 1:L + 1], ps[:, 0:N])
            elif mt < 14:
                b.evac(zlx[:, mt - 12, 0:nseq * (L + 3)].rr("p (s t) -> p s t", t=L + 3)[:, :, 3:L + 3], pcur(ps))
            elif mt < 16:
                b.evac(sl(zlg[:, mt - 14, 0:N]), pcur(ps))
            else:
                b.evac(sl(zs5[:, mt - 16, 0:N]), pcur(ps))
                b.cp(zs5b[:, mt - 16, 0:N], zs5[:, mt - 16, 0:N], "pool")
        proj(w_in[l], 8, DIN, rhs_fn, NP, ev_in, key='w_in%d' % l)
        if not smp:
            for mt in range(12):
                b.cp(zrkv[:, mt, 0:1], zlast[:, l, mt, :], "pool")
            for mt in range(12):
                b.cp(zlast[:, l, mt, :], zrkv[:, mt, L:L + 1], "pool")

        tg('lru')
        if do_lru:
            lru_stage(blk, l)
        else:
            b.memset(ycat[:, 4:6, :], 0.0)
        tg('s5')
        if do_s5:
            s5_stage(blk, l)
        else:
            b.memset(ycat[:, 6:8, :], 0.0)
        tg('rw')
        if do_rwkv:
            for p in range(4 if RWL >= 2 else 0):
                rwkv_pair(blk, l, p)
            if RWL < 2:
                b.memset(ycat[:, 0:4, :], 0.0)
        else:
            b.memset(ycat[:, 0:4, :], 0.0)
        if "ycat%d" % l in taps and pbi == 0:
            for kt in range(8):
                b.cp(arena[:, kt, 0:N], ycat[:, kt, 0:N])
            b.tap("ycat%d" % l, arena[:, 0:8, 0:N], [128, 8, N])

        tg('wout')
        def ev_res(mt, ps):
            b.tt(xT[:, mt, 0:N], xT[:, mt, 0:N], ps[:, 0:N], ALU.add)
        proj(w_out[l], 8, D, lambda kt: ycat[:, kt, 0:N], N, ev_res, key='w_out%d' % l)
        if "x1_%d" % l in taps and pbi == 0:
            b.tap("x1_%d" % l, xT[:, :, 0:N], [128, 8, N])
        tg('attn')
        if do_attn:
            attn_stage(blk, l)
        if "x2_%d" % l in taps and pbi == 0:
            b.tap("x2_%d" % l, xT[:, :, 0:N], [128, 8, N])
        tg('ffn')
        if do_ffn:
            rmsnorm(lambda kt: xT[:, kt, 0:N], N, "norm_ffn", lambda kt: hxb[:, kt, 0:N])

            hidv = lambda mt: hid2[:, mt // 2, (mt % 2) * 512:(mt % 2) * 512 + N]

            def ev_up(mt, ps):
                if mt < 22:
                    b.act(hidv(mt), ps[:, 0:N], AF.Silu)
                else:
                    b.tt(hidv(mt - 22), hidv(mt - 22), ps[:, 0:N], ALU.mult)
            proj(w_up[l], 8, 2 * DFF, lambda kt: hxb[:, kt, 0:N], N, ev_up, key='w_up%d' % l)
            proj(w_down[l], 22, D, hidv, N, ev_res, key='w_down%d' % l)

    def lru_stage(blk, l):
        nseq, L, N, pbi = blk["nseq"], blk["L"], blk["N"], blk["pbi"]
        smp = pbi is None
        sl = lambda tv: tv.rr("p (s t) -> p s t", t=L)
        ar = lambda i: arena[:, i, 0:N]
        for t in range(2):
            zxe = zlx[:, t, 0:nseq * (L + 3)].rr("p (s t) -> p s t", t=L + 3)
            if smp:
                b.dma(stg[:, 0:48], sconv[l, t * 128:(t + 1) * 128, :])
                b.cp(zxe[:, :, 0:3], stg[:, 0:48].rr("p (s t) -> p s t", t=3), "pool")
                b.dma(stg[:, 64:80], slru[l, t * 128:(t + 1) * 128, :])
                h0v = stg[:, 64:80].rr("p (s o) -> p s o", o=1)
            else:
                b.cp(zxe[:, 0, 0:3], clast[:, l, t, :], "pool")
                h0v = hlru[:, l, t, :].rr("p (s o) -> p s o", o=1)
            xc = sl(ar(0))
            b.ts(xc, zxe[:, :, 0:L], pvc("conv_w0", t), pvc("conv_b", t), ALU.mult, ALU.add)
            for j in range(1, 4):
                b.stt(xc, zxe[:, :, j:j + L], pvc("conv_w%d" % j, t), xc, ALU.mult, ALU.add)
            if smp:
                b.cp(stg2[:, 0:48].rr("p (s t) -> p s t", t=3), zxe[:, :, L:L + 3], "pool")
                b.dma(convs[l, :, t * 48:(t + 1) * 48], stg2[:, 0:48])
            else:
                b.cp(clast[:, l, t, :], zxe[:, 0, L:L + 3], "pool")
                if pbi == NPB - 1:
                    b.dma(convp[l, :, t * 3:(t + 1) * 3], clast[:, l, t, :])
            xcb = bR[:, 0, 0:N]
            b.cp(xcb, ar(0))
            ps = bank()
            b.mm(ps[:, 0:N], lab[:, t, :], xcb)
            b.act(ar(1), ps[:, 0:N], AF.Sigmoid, bias=pvc("lru_ba", t))
            ps = bank()
            b.mm(ps[:, 0:N], lib[:, t, :], xcb)
            b.act(ar(2), ps[:, 0:N], AF.Sigmoid, bias=pvc("lru_bi", t))
            b.act(ar(3), ar(1), AF.Exp, scale=pvx[:, 6 + t:7 + t])
            b.act(ar(4), ar(1), AF.Exp, scale=pvx[:, 8 + t:9 + t])
            b.ts(ar(4), ar(4), -1.0, 1.0, ALU.mult, ALU.add)
            b.act(ar(4), ar(4), AF.Sqrt)
            if (not smp) and pbi == 0:
                b.memset(arena[:, 4, 0:1], 1.0, "dve")
            b.tt(ar(4), ar(4), ar(2), ALU.mult)
            b.tt(ar(4), ar(4), ar(0), ALU.mult)
            a3 = sl(ar(3))[:, :, 0:1]
            b3 = sl(ar(4))[:, :, 0:1]
            tmp = arena[:, 7, 0:nseq].rr("p (s o) -> p s o", o=1)
            b.tt(tmp, a3, h0v, ALU.mult)
            b.tt(b3, b3, tmp, ALU.add)
            b.memset(a3, 0.0, "dve")
            b.scan(ar(5), ar(3), ar(4), 0.0)
            if smp:
                b.cp(stg2[:, 64:80].rr("p (s o) -> p s o", o=1), sl(ar(5))[:, :, L - 1:L], "pool")
                b.dma(lrus[l, :, t * NSS:(t + 1) * NSS], stg2[:, 64:80])
            else:
                b.cp(hlru[:, l, t, :], arena[:, 5, N - 1:N], "pool")
                if pbi == NPB - 1:
                    b.dma(lrup[l, t, :].rr("(p o) -> p o", o=1), hlru[:, l, t, :])
            b.act(ar(6), zlg[:, t, 0:N], AF.Gelu_apprx_tanh)
            b.tt(ycat[:, 4 + t, 0:N], ar(5), ar(6), ALU.mult)

    def s5_stage(blk, l):
        nseq, L, N, pbi = blk["nseq"], blk["L"], blk["N"], blk["pbi"]
        smp = pbi is None
        nsc, Lsc = (nseq, L) if smp else (N // STAB, STAB)
        sc_ = lambda tv: tv.rr("p (s t) -> p s t", t=Lsc)
        ar = lambda i: arena[:, i, 0:N]
        if smp:
            for ri in range(2):
                b.dma(stg[:].rr("p (c s) -> p c s", c=8) if ri == 0 else stg2[:].rr("p (c s) -> p c s", c=8),
                      ss5[ri][l].rr("p (c s) -> p c s", c=8))
            sin_ = [stg, stg2]
        for ct in range(8):
            ut, ot = ct // 4, ct // 4
            rhs = zs5b[:, ut, 0:N]
            psr = bank()
            b.mm(psr[:, 0:N], Bb[:, 0, ct, :], rhs)
            psi = bank()
            b.mm(psi[:, 0:N], Bb[:, 1, ct, :], rhs)
            tb_ = lambda k: tabs[:, ct, k:k + 1, 0:Lsc].bc([128, nsc, Lsc])
            Ec, Es, Fc, Fs = tb_(0), tb_(1), tb_(2), tb_(3)
            dre, dim_, t2 = sc_(ar(0)), sc_(ar(1)), sc_(ar(2))
            b.tt(dre, sc_(psr[:, 0:N]), Fc, ALU.mult)
            b.tt(t2, sc_(psi[:, 0:N]), Fs, ALU.mult)
            b.tt(dre, dre, t2, ALU.subtract)
            b.tt(dim_, sc_(psi[:, 0:N]), Fc, ALU.mult)
            b.tt(t2, sc_(psr[:, 0:N]), Fs, ALU.mult)
            b.tt(dim_, dim_, t2, ALU.add)
            zre, zim = sc_(ar(3)), sc_(ar(4))
            c1 = s5c[:, 3, ct:ct + 1]; s1 = s5c[:, 4, ct:ct + 1]
            cL = s5u[:, ct, 0, LSUB:LSUB + 1]; sL = s5u[:, ct, 1, LSUB:LSUB + 1]
            zi = arena[:, 7, 0:4 * NSS].rr("p (k s) -> p k s", k=4)

            def rot(o_re, o_im, x_re, x_im, cc, ss, t_a, t_b):
                b.ts(t_a, x_im, ss, None, ALU.mult)
                b.stt(o_re, x_re, cc, t_a, ALU.mult, ALU.subtract)
                b.ts(t_b, x_im, cc, None, ALU.mult)
                b.stt(o_im, x_re, ss, t_b, ALU.mult, ALU.add)
            if smp:
                xre0 = sin_[0][:, ct * NSS:(ct + 1) * NSS]; xim0 = sin_[1][:, ct * NSS:(ct + 1) * NSS]
                rot(zi[:, 0, :], zi[:, 1, :], xre0, xim0, c1, s1, zi[:, 2, :], zi[:, 3, :])
                for s in range(nsc):
                    b.scan(zre[:, s, :], rhot[:, ct, 0:Lsc], dre[:, s, :], zi[:, 0, s:s + 1])
                    b.scan(zim[:, s, :], rhot[:, ct, 0:Lsc], dim_[:, s, :], zi[:, 1, s:s + 1])
            else:
                S16 = nsc
                sm_ = arena[:, 7, 0:12 * 17].rr("p (k s) -> p k s", k=12)
                b.ts(ar(5), rmask[STAB][:, 0:N], s5c[:, 2, ct:ct + 1], None, ALU.mult)
                b.scan(ar(3), ar(5), ar(0), 0.0)
                b.scan(ar(4), ar(5), ar(1), 0.0)
                ere = zre[:, :, Lsc - 1:Lsc].rr("p s o -> p (s o)"); eim = zim[:, :, Lsc - 1:Lsc].rr("p s o -> p (s o)")
                gc_ = gtab[:, ct, 0, :]; gs_ = gtab[:, ct, 1, :]
                g_re = sm_[:, 0, 0:S16]; g_im = sm_[:, 1, 0:S16]; ta = sm_[:, 2, 0:S16]; tb2 = sm_[:, 3, 0:S16]
                b.tt(ta, gc_, ere, ALU.mult)
                b.tt(tb2, gs_, eim, ALU.mult)
                b.tt(g_re, ta, tb2, ALU.add)
                b.tt(ta, gc_, eim, ALU.mult)
                b.tt(tb2, gs_, ere, ALU.mult)
                b.tt(g_im, ta, tb2, ALU.subtract)
                qre = sm_[:, 4, :]; qim = sm_[:, 5, :]
                rot(qre[:, 0:1], qim[:, 0:1], s5st[:, l, 0, ct:ct + 1], s5st[:, l, 1, ct:ct + 1], c1, s1,
                    sm_[:, 6, 0:1], sm_[:, 7, 0:1])
                rl = sm_[:, 8, 0:S16]
                b.ts(rl, onesf[:, 0:S16], s5c[:, 10, ct:ct + 1], None, ALU.mult)
                b.scan(qre[:, 1:S16 + 1], rl, g_re, qre[:, 0:1])
                b.scan(qim[:, 1:S16 + 1], rl, g_im, qim[:, 0:1])
                zr_ = sm_[:, 9, 0:S16]; zi_ = sm_[:, 10, 0:S16]
                b.tt(ta, gc_, qre[:, 0:S16], ALU.mult)
                b.tt(tb2, gs_, qim[:, 0:S16], ALU.mult)
                b.tt(zr_, ta, tb2, ALU.subtract)
                b.tt(ta, gc_, qim[:, 0:S16], ALU.mult)
                b.tt(tb2, gs_, qre[:, 0:S16], ALU.mult)
                b.tt(zi_, ta, tb2, ALU.add)
                rp = rpow[:, ct:ct + 1, 0:Lsc].bc([128, nsc, Lsc])
                b.tt(sc_(ar(6)), rp, zr_.rr("p (s o) -> p s o", o=1).bc([128, nsc, Lsc]), ALU.mult)
                b.tt(ar(3), ar(3), ar(6), ALU.add)
                b.tt(sc_(ar(6)), rp, zi_.rr("p (s o) -> p s o", o=1).bc([128, nsc, Lsc]), ALU.mult)
                b.tt(ar(4), ar(4), ar(6), ALU.add)
            xr, xi = sc_(bR[:, 0, 0:N]), sc_(bR[:, 1, 0:N])
            t5, t6 = sc_(ar(5)), sc_(ar(6))
            b.tt(t5, zre, Ec, ALU.mult)
            b.tt(t6, zim, Es, ALU.mult)
            b.tt(xr, t5, t6, ALU.subtract)
            b.tt(t5, zre, Es, ALU.mult)
            b.tt(t6, zim, Ec, ALU.mult)
            b.tt(xi, t5, t6, ALU.add)
            eC = tabs[:, ct, 0, Lsc - 1:Lsc]; eS = tabs[:, ct, 1, Lsc - 1:Lsc]
            if smp:
                zl_re = zre[:, :, Lsc - 1:Lsc].rr("p s o -> p (s o)"); zl_im = zim[:, :, Lsc - 1:Lsc].rr("p s o -> p (s o)")
                o_re = arena[:, 8, ct * NSS:(ct + 1) * NSS]; o_im = arena[:, 9, ct * NSS:(ct + 1) * NSS]
                rot(o_re, o_im, zl_re, zl_im, eC, eS, zi[:, 2, :], zi[:, 3, :])
            else:
                rot(s5st[:, l, 0, ct:ct + 1], s5st[:, l, 1, ct:ct + 1], arena[:, 3, N - 1:N], arena[:, 4, N - 1:N], eC, eS,
                    zi[:, 2, 0:1], zi[:, 3, 0:1])
            b.mm(pb[ot][:, 0:N], Cb[:, 0, ct, :], bR[:, 0, 0:N], ct % 4 == 0, False)
            b.mm(pb[ot][:, 0:N], Cb[:, 1, ct, :], bR[:, 1, 0:N], False, ct % 4 == 3)
        if smp:
            for ri in range(2):
                b.dma(s5s[ri][l], arena[:, 8 + ri, 0:8 * NSS])
        elif pbi == NPB - 1:
            for ri in range(2):
                b.dma(s5p[ri][l], s5st[:, l, ri, :])
        for ot in range(2):
            b.stt(ar(10), zs5[:, ot, 0:N], pvc("s5_d", ot), pb[ot][:, 0:N], ALU.mult, ALU.add)
            b.act(bR[:, 2 + ot, 0:N], ar(10), AF.Gelu_apprx_tanh)

        def ev_glu(mt, ps):
            if mt < 2:
                b.act(ar(11 + mt), ps[:, 0:N], AF.Identity, bias=pvc("b_glu", mt))
            else:
                b.act(ar(13), ps[:, 0:N], AF.Sigmoid, bias=pvc("b_glu", mt))
                b.tt(ycat[:, 6 + mt - 2, 0:N], ar(11 + mt - 2), ar(13), ALU.mult)
        proj(w_glu[l], 2, 512, lambda kt: bR[:, 2 + kt, 0:N], N, ev_glu)

    def rwkv_pair(blk, l, p):
        nseq, L, N, pbi = blk["nseq"], blk["L"], blk["N"], blk["pbi"]
        smp = pbi is None
        EN = nseq * (L + 1)
        C = L if smp else 64
        nch = N // C
        nlev = int(np.log2(C))
        ext = lambda tv: tv.rr("p (s t) -> p s t", t=L + 1)
        sl = lambda tv: tv.rr("p (s t) -> p s t", t=L)
        ar = lambda i: arena[:, i, 0:N]
        pcs = slice(p * 128, (p + 1) * 128)

        def mixz(dst, mt, mun):
            ze = ext(zrkv[:, mt, 0:EN])
            b.tt(sl(ar(11)), ze[:, :, 0:L], ze[:, :, 1:L + 1], ALU.subtract)
            b.stt(sl(dst), sl(ar(11)), pvc(mun, p), ze[:, :, 1:L + 1], ALU.mult, ALU.add)
        _TAG[0] = 'rw_prep'
        mixz(ar(0), p, "mu_r")
        mixz(ar(1), 4 + p, "mu_k")
        v = vfirst[:, p, 0:N] if l == 0 else ar(2)
        mixz(v, 8 + p, "mu_v")
        ps = bank()
        b.mm(ps[:, 0:N], w2b[0:64, pcs], tl[0:64, 0, 0:N])
        b.act(ar(3), ps[:, 0:N], AF.Sigmoid, bias=pvc("w0", p))
        b.ts(ar(3), ar(3), -0.6065306597126334, None, ALU.mult)
        ps = bank()
        b.mm(ps[:, 0:N], a2b[0:64, pcs], tl[0:64, 1, 0:N])
        b.act(ar(4), ps[:, 0:N], AF.Sigmoid, bias=pvc("a0", p))
        ps = bank()
        b.mm(ps[:, 0:N], g2b[:, 0, pcs], tl[:, 3, 0:N], True, False)
        b.mm(ps[:, 0:N], g2b[:, 1, pcs], tl[:, 4, 0:N], False, True)
        b.evac(ar(14), ps[:, 0:N])
        if l >= 1:
            ps = bank()
            b.mm(ps[:, 0:N], v2b[0:32, pcs], tl[0:32, 2, 0:N])
            b.act(ar(15), ps[:, 0:N], AF.Sigmoid, bias=pvc("v0", p))
            b.tt(ar(11), vfirst[:, p, 0:N], ar(2), ALU.subtract)
            b.tt(ar(11), ar(11), ar(15), ALU.mult)
            b.tt(ar(2), ar(2), ar(11), ALU.add)
        b.ts(ar(5), ar(1), pvc("k_k", p), None, ALU.mult)
        b.act(ar(11), ar(5), AF.Square)
        ps = bank()
        b.mm(ps[:, 0:N], blk1[:], ar(11))
        b.act(ar(12), ps[:, 0:N], AF.Sqrt)
        b.ts(ar(12), ar(12), 1e-12, None, ALU.max)
        b.recip(ar(12), ar(12))
        b.tt(ar(5), ar(5), ar(12), ALU.mult)
        b.ts(ar(11), ar(4), pvc("k_a", p), pvx[:, p:p + 1], ALU.mult, ALU.add)
        b.tt(ar(6), ar(1), ar(11), ALU.mult)
        b.tt(ar(7), ar(5), ar(4), ALU.mult)
        b.stt(ar(11), ar(0), pvc("r_k", p), ar(6), ALU.mult, ALU.mult)
        ps = bank()
        b.mm(ps[:, 0:N], blk1[:], ar(11))
        b.tt(ar(15), ps[:, 0:N], v, ALU.mult)
        b.scan(ar(8), rmask[C][:, 0:N], ar(3), 0.0)
        b.act(ar(9), ar(8), AF.Exp)
        b.tt(bR[:, 1, 0:N], ar(0), ar(9), ALU.mult)
        b.act(ar(9), ar(8), AF.Exp, scale=-1.0)
        b.tt(bR[:, 2, 0:N], ar(7), ar(9), ALU.mult)
        b.tt(bR[:, 3, 0:N], ar(6), ar(9), ALU.mult)
        b.tt(ar(10), ar(8), ar(3), ALU.subtract)
        b.act(ar(10), ar(10), AF.Exp)
        b.stt(bR[:, 0, 0:N], ar(5), -1.0, ar(10), ALU.mult, ALU.mult)
        c8v = ar(8).rr("p (c k) -> p c k", k=C)
        b.tt(ar(10).rr("p (c k) -> p c k", k=C), c8v[:, :, C - 1:C].bc([128, nch, C]), c8v, ALU.subtract)
        b.act(ar(10), ar(10), AF.Exp)
        b.tt(bR[:, 4, 0:N], ar(7), ar(10), ALU.mult)
        b.tt(bR[:, 5, 0:N], ar(6), ar(10), ALU.mult)
        b.act(gC[:, 0:nch].rr("p (c o) -> p c o", o=1), c8v[:, :, C - 1:C], AF.Exp)
        b.cp(bR[:, 6, 0:N], v, "pool")
        _TAG[0] = 'rw_chunk'
        G = 2
        ngrp = nch // G
        C4 = 4 * C

        def cols_of(ch):
            return slice(ch * C, (ch + 1) * C)

        def tmv(st_, q, k, hh):
            return mx[0:C, st_ * 2 + q, k * 128 + hh * 64:k * 128 + (hh + 1) * 64]

        def Mblk(st_, q, hh, blk):
            return mx[0:C, 4 + st_ * 2 + q, (hh * 4 + blk) * C:(hh * 4 + blk + 1) * C]

        def v4(tv, t):
            return tv.rr("c (q h t) -> c q h t", q=2, h=2)

        def A_steps(g):
            st_ = g % 2
            bA, bB, bG = pb[2 * st_], pb[2 * st_ + 1], pg[st_]
            bk = [bA, bB]
            steps = []

            def s_tr():
                for q in range(G):
                    cols = cols_of(g * G + q)
                    for k, src in enumerate((6, 4, 5, 0)):
                        b.tr(ptr[0:C, (q * 4 + k) * 128:(q * 4 + k + 1) * 128], bR[:, src, cols], ident[:])
                b.cp(mx[0:C, st_ * 2:st_ * 2 + 2, :], ptr[0:C, 0:1024].rr("c (q x) -> c q x", q=2), "act")
            steps.append(s_tr)

            def s_M():
                for q in range(G):
                    cols = cols_of(g * G + q)
                    for hh in range(2):
                        hs = slice(hh * 64, hh * 64 + 64)
                        for j in range(2):
                            b.mm(bk[hh][0:C, q * C4 + j * C:q * C4 + (j + 1) * C], bR[hs, 2, cols], bR[hs, j, cols])
                            b.mm(bk[hh][0:C, q * C4 + (2 + j) * C:q * C4 + (3 + j) * C], bR[hs, 3, cols], bR[hs, j, cols])
                for hh in range(2):
                    b.tt(mx[0:C, 4 + st_ * 2:6 + st_ * 2, hh * C4:(hh + 1) * C4],
                         bk[hh][0:C, 0:2 * C4].rr("c (q x) -> c q x", q=2),
                         m4[C][:, hh:hh + 1, :, :].rr("c o k t -> c o (k t)").bc([C, 2, C4]), ALU.mult)
            steps.append(s_M)

            def s_NT():
                for q in range(G):
                    for hh in range(2):
                        b.tr(ptr[0:C, (q * 2 + hh) * 64:(q * 2 + hh) * 64 + C], Mblk(st_, q, hh, 0), ident[0:C, 0:C])
                b.cp(v4(NTG[0:C, st_, :], 64)[:, :, :, 0:C], v4(ptr[0:C, 0:256], 64)[:, :, :, 0:C], "act")
                for q in range(G):
                    for hh in range(2):
                        b.mm(bG[0:C, (q * 2 + hh) * 64:(q * 2 + hh + 1) * 64], Mblk(st_, q, hh, 2), tmv(st_, q, 0, hh))
                for q in range(G):
                    X0q = XbG[0:C, q, :].rr("c (h x) -> c h x", h=2)
                    b.cp(X0q[:, :, 64:128], bG[0:C, q * 128:(q + 1) * 128].rr("c (h i) -> c h i", h=2), "act")
                    b.cp(X0q[:, :, 0:64], mx[0:C, st_ * 2 + q, 384:512].rr("c (h j) -> c h j", h=2), "act")
            steps.append(s_NT)

            def mk_level(k):
                def s_lvl():
                    xp = k % 2
                    bXq = [bA, pg[2]]
                    bSq = [pg[st_], pg[1 - st_]]
                    for q in range(G):
                        Xk = XbG[0:C, xp * 2 + q, :].rr("c (h x) -> c h x", h=2)
                        if k == 0:
                            Pk = lambda hh, q=q: Mblk(st_, q, hh, 0)
                            PTk = lambda hh, q=q: v4(NTG[0:C, st_, :], 64)[:, q, hh, 0:C]
                        else:
                            Pv = PbG[0:C, ((k - 1) % 2) * 2 + q, :].rr("c (h e t) -> c h e t", h=2, e=2)
                            Pk = lambda hh, Pv=Pv: Pv[:, hh, 0, 0:C]
                            PTk = lambda hh, Pv=Pv: Pv[:, hh, 1, 0:C]
                        for hh in range(2):
                            o_ = bXq[q][0:C, hh * 128:(hh + 1) * 128]
                            b.mm(o_, ident[0:C, 0:C], Xk[:, hh, :], True, False)
                            b.mm(o_, Pk(hh), Xk[:, hh, :], False, True)
                        if k < nlev - 1:
                            for hh in range(2):
                                o = (hh * 2) * 64
                                b.mm(bSq[q][0:C, o:o + C], PTk(hh), Pk(hh))
                                b.mm(bSq[q][0:C, o + 64:o + 64 + C], Pk(hh), PTk(hh))
                    for q in range(G):
                        b.cp(XbG[0:C, (1 - xp) * 2 + q, :], bXq[q][0:C, 0:256], "act")
                        if k < nlev - 1:
                            Pn_ = PbG[0:C, (k % 2) * 2 + q, :].rr("c (a t) -> c a t", t=64)[:, :, 0:C]
                            b.cp(Pn_, bSq[q][0:C, 0:256].rr("c (a t) -> c a t", t=64)[:, :, 0:C])
                    if k == nlev - 1:
                        for q in range(G):
                            XA = XbG[0:C, (1 - xp) * 2 + q, :].rr("c (h x) -> c h x", h=2)
                            b.cp(Z1c[0:C, q * 128:(q + 1) * 128].rr("c (h j) -> c h j", h=2), XA[:, :, 0:64], "pool")
                            b.cp(Z2f[0:C, st_, q * 128:(q + 1) * 128].rr("c (h i) -> c h i", h=2), XA[:, :, 64:128], "pool")
                return s_lvl
            for k in range(nlev):
                steps.append(mk_level(k))

            def s_fin():
                for q in range(G):
                    b.tr(ptr[:, q * 64:q * 64 + C], Z1c[0:C, q * 128:(q + 1) * 128], ident[0:C, 0:C])
                b.cp(Z1T[:, st_, :].rr("p (q t) -> p q t", q=2)[:, :, 0:C], ptr[:, 0:128].rr("p (q t) -> p q t", q=2)[:, :, 0:C], "act")
                for q in range(G):
                    for hh in range(2):
                        hs = slice(hh * 64, hh * 64 + 64)
                        b.mm(bG[hs, q * 64:(q + 1) * 64], tmv(st_, q, 2, hh), tmv(st_, q, 0, hh))
                b.cp(KVf[:, st_, :], bG[:, 0:128])
            steps.append(s_fin)
            return steps

        def S_io(ch):
            if smp:
                return sw[:, (ch % 2), :], sw[:, 2 + (ch % 2), :]
            return ST[:, l, p, :], ST[:, l, p, :]

        def B_steps(g):
            st_ = g % 2
            bA, bB = pb[2 * st_], pb[2 * st_ + 1]
            bk = [bA, bB]
            steps = []
            for q in range(G):
                ch = g * G + q
                cols = cols_of(ch)
                s2 = ch % 2
                si, so = S_io(ch)
                UT = lambda hh, s2=s2: Xb[0:C, s2, hh * 64:(hh + 1) * 64]

                def s_u(ch=ch, q=q, cols=cols, s2=s2, si=si, so=so):
                    if smp:
                        b.dma(si, swkv[l, :, ch * 256 + p * 64: ch * 256 + (p + 1) * 64])
                    if smp or ch == 0:
                        b.cp(STb[:, s2, :], si, "act")
                    b.stt(Stmp[:, s2, :], si, gC[:, ch:ch + 1], KVf[:, st_, q * 64:(q + 1) * 64], ALU.mult, ALU.add)
                    for hh in range(2):
                        hs = slice(hh * 64, hh * 64 + 64)
                        b.mm(bk[hh][0:C, 0:64], Z1T[hs, st_, q * 64:q * 64 + C], STb[hs, s2, :])
                    for hh in range(2):
                        b.tt(Xb[0:C, s2, hh * 64:(hh + 1) * 64], bk[hh][0:C, 0:64],
                             v4(Z2f[0:C, st_, :], 64)[:, q, hh, :], ALU.add)
                steps.append(s_u)

                def s_s(ch=ch, q=q, cols=cols, s2=s2, si=si, so=so, UT=UT):
                    for hh in range(2):
                        hs = slice(hh * 64, hh * 64 + 64)
                        b.mm(bA[hs, 64:128], tmv(st_, q, 1, hh), UT(hh))
                    if not smp:
                        b.tt(STb[:, 1 - s2, :], Stmp[:, s2, :], bA[:, 64:128], ALU.add)
                    b.tt(so, Stmp[:, s2, :], bA[:, 64:128], ALU.add)
                    if smp:
                        b.dma(wkvs[l, :, ch * 256 + p * 64: ch * 256 + (p + 1) * 64], so)
                steps.append(s_s)

                def s_y(ch=ch, q=q, cols=cols, s2=s2, UT=UT):
                    b.mm(bA[0:64, 128:128 + C], STb[0:64, s2, :], bR[0:64, 1, cols])
                    b.mm(bB[64:128, 128:128 + C], STb[64:128, s2, :], bR[64:128, 1, cols])
                    for hh in range(2):
                        hs = slice(hh * 64, hh * 64 + 64)
                        b.mm(bA[hs, 192:192 + C], UT(hh), Mblk(st_, q, hh, 1))
                        b.mm(bA[hs, 256:256 + C], tmv(st_, q, 0, hh), Mblk(st_, q, hh, 3))
                    b.cp(arena[0:64, 13, cols], bA[0:64, 128:128 + C], "act")
                    b.cp(arena[64:128, 13, cols], bB[64:128, 128:128 + C], "act")
                    b.tt(arena[:, 13, cols], arena[:, 13, cols], bA[:, 192:192 + C], ALU.add)
                    b.tt(arena[:, 13, cols], arena[:, 13, cols], bA[:, 256:256 + C], ALU.add)
                steps.append(s_y)
            if len(steps) == 6:
                steps = [steps[0], steps[1], steps[3], steps[2], steps[4], steps[5]]
            return steps

        if RWL >= 3 and nch > 0:
            ASTOP = int(os.environ.get("ASTOP", "99")); BSTOP = int(os.environ.get("BSTOP", "3"))
            for f in A_steps(0)[:ASTOP]:
                f()
            for g in range(ngrp):
                As = A_steps(g + 1)[:ASTOP] if g + 1 < ngrp else []
                Bs = B_steps(g)
                for i_ in range(max(len(As), len(Bs))):
                    if i_ < len(Bs):
                        Bs[i_]()
                    if i_ < len(As):
                        As[i_]()
        if (not smp) and pbi == NPB - 1:
            b.dma(wkvp[l, :, p * 64:(p + 1) * 64], ST[:, l, p, :])
        _TAG[0] = 'rw_post'
        ps = bank()
        b.mm(ps[:, 0:N], blk1[:], ar(13))
        b.stt(ar(9), ps[:, 0:N], -1.0 / 64, ar(13), ALU.mult, ALU.add)
        b.act(ar(10), ar(9), AF.Square)
        ps = bank()
        b.mm(ps[:, 0:N], blk1[:], ar(10))
        b.act(ar(10), ps[:, 0:N], AF.Sqrt, bias=pvx[:, 5:6], scale=1.0 / 64)
        b.recip(ar(10), ar(10))
        b.tt(ar(9), ar(9), ar(10), ALU.mult)
        b.ts(ar(9), ar(9), pvc("ln_w", p), pvc("ln_b", p), ALU.mult, ALU.add)
        b.tt(ar(9), ar(9), ar(15), ALU.add)
        b.tt(ycat[:, p, 0:N], ar(9), ar(14), ALU.mult)

    def attn_stage(blk, l):
        nseq, L, N, pbi = blk["nseq"], blk["L"], blk["N"], blk["pbi"]
        smp = pbi is None
        rmsnorm(lambda kt: xT[:, kt, 0:N], N, "norm_mem_q", lambda kt: hxb[:, kt, 0:N])
        proj(wq[l], 8, D, lambda kt: hxb[:, kt, 0:N], N, lambda mt, ps: b.evac(qb[:, mt, 0:N], ps[:, 0:N]), key='wq%d' % l)
        SC = 1.0 / 16.0
        if not smp:
            for h in range(4):
                for tt_ in range(N // 128):
                    tcs = slice(tt_ * 128, (tt_ + 1) * 128)
                    ps = bank()
                    for dt in range(2):
                        b.mm(ps[:, 0:NMEM], qb[:, 2 * h + dt, tcs], kT[:, l, 2 * h + dt, :], dt == 0, dt == 1)
                    i = tt_ % 2
                    pe = arena[:, i, 0:NMEM]
                    b.rmax(sm[:, 0:1], ps[:, 0:NMEM])
                    b.ts(sm[:, 1:2], sm[:, 0:1], -SC, None, ALU.mult)
                    b.act(pe, ps[:, 0:NMEM], AF.Exp, bias=sm[:, 1:2], scale=SC)
                    b.rsum(sm[:, 2:3], pe)
                    b.recip(sm[:, 3:4], sm[:, 2:3])
                    b.ts(Pn[:, i, :], pe, sm[:, 3:4], None, ALU.mult)
                    for mt in range(2):
                        b.tr(ptr[:, mt * 128:(mt + 1) * 128], Pn[:, i, mt * 128:(mt + 1) * 128], ident[:])
                    b.evac(PTb[:, :, tcs], ptr[:, 0:256].rr("m (k t) -> m k t", k=2))
                for dt in range(2):
                    ps = bank()
                    for mt in range(2):
                        b.mm(ps[:, 0:N], vM[:, l, mt, h * 256 + dt * 128:h * 256 + (dt + 1) * 128], PTb[:, mt, 0:N], mt == 0, mt == 1)
                    b.evac(ycat[:, 2 * h + dt, 0:N], ps[:, 0:N])
        else:
            for s in range(nseq):
                qc = slice(s * L, (s + 1) * L)
                b.dma(ckb[:], ck[l, s].rr("p (a x) -> p a x", a=2), q="pool")
                b.dma(cvb[:], cv[l, s].rr("p (a x) -> p a x", a=2), q="pool")
                pss = [bank(), bank()]
                for h in range(4):
                    for dt in range(2):
                        b.mm(pss[h // 2][0:L, (h % 2) * 256:(h % 2 + 1) * 256], qb[:, 2 * h + dt, qc],
                             ckb[:, (h * 2 + dt) // 4, ((h * 2 + dt) % 4) * 256:((h * 2 + dt) % 4 + 1) * 256], dt == 0, dt == 1)
                for hf in range(2):
                    b.rmax(sm[0:L, hf * 2:hf * 2 + 2], pss[hf][0:L, 0:512].rr("q (h m) -> q h m", h=2))
                b.ts(sm[0:L, 4:8], sm[0:L, 0:4], -SC, None, ALU.mult)
                for h in range(4):
                    b.act(arena[0:L, h // 2, (h % 2) * 256:(h % 2 + 1) * 256], pss[h // 2][0:L, (h % 2) * 256:(h % 2 + 1) * 256],
                          AF.Exp, bias=sm[0:L, 4 + h:5 + h], scale=SC)
                for hf in range(2):
                    peh = arena[0:L, hf, 0:512].rr("q (h m) -> q h m", h=2)
                    b.rsum(sm[0:L, hf * 2:hf * 2 + 2], peh)
                b.recip(sm[0:L, 0:4], sm[0:L, 0:4])
                for hf in range(2):
                    peh = arena[0:L, hf, 0:512].rr("q (h m) -> q h m", h=2)
                    pnh = PTb[0:L, hf, 0:512].rr("q (h m) -> q h m", h=2)
                    b.tt(pnh, peh, sm[0:L, hf * 2:hf * 2 + 2].rr("q (h o) -> q h o", o=1).bc([L, 2, NMEM]), ALU.mult)
                for h in range(4):
                    for mt in range(2):
                        b.tr(ptr[:, (h * 2 + mt) * L:(h * 2 + mt + 1) * L],
                             PTb[0:L, h // 2, (h % 2) * 256 + mt * 128:(h % 2) * 256 + (mt + 1) * 128], ident[0:L, 0:L])
                pt8 = Pn[:, 0, 0:8 * L]
                b.evac(pt8, ptr[:, 0:8 * L])
                ps = bank()
                for h in range(4):
                    for dt in range(2):
                        for mt in range(2):
                            b.mm(ps[:, (h * 2 + dt) * L:(h * 2 + dt + 1) * L],
                                 cvb[:, mt, h * 256 + dt * 128:h * 256 + (dt + 1) * 128],
                                 Pn[:, 0, (h * 2 + mt) * L:(h * 2 + mt + 1) * L], mt == 0, mt == 1)
                b.evac(ycat[:, :, qc], ps[:, 0:8 * L].rr("d (k q) -> d k q", q=L))

        def ev_res(mt, ps):
            b.tt(xT[:, mt, 0:N], xT[:, mt, 0:N], ps[:, 0:N], ALU.add)
        proj(wo[l], 8, D, lambda kt: ycat[:, kt, 0:N], N, ev_res, key='wo%d' % l)

    if do_attn and any(bi < NPB for bi in n_blocks):
        for l in range(n_layers):
            mem_phase(l)
    for bi in n_blocks:
        if bi < NPB:
            blk = {"nseq": 1, "L": TB, "N": TB, "pbi": bi}
            for kt in range(8):
                b.dma(xT[:, kt, :], xp[kt * 128:(kt + 1) * 128, bi * TB:(bi + 1) * TB])
        else:
            blk = {"nseq": NSS, "L": LS, "N": NSS * LS, "pbi": None}
            for kt in range(8):
                b.dma(xT[:, kt, 0:128], xs[kt * 128:(kt + 1) * 128, :])
        N = blk["N"]
        for l in range(n_layers):
            load_layer_consts(l)
            layer(blk, l)
        first_pass[0] = False
        _TAG[0] = 'final'
        rmsnorm(lambda kt: xT[:, kt, 0:N], N, "norm_final", lambda kt: arena[:, kt, 0:N])
        for kt in range(8):
            if bi < NPB:
                b.dma(yp[kt * 128:(kt + 1) * 128, bi * TB:(bi + 1) * TB], arena[:, kt, 0:N])
            else:
                b.dma(ys[kt * 128:(kt + 1) * 128, :], arena[:, kt, 0:N])


def make_in_maps(inp, cores=None):
    f = lambda a: np.ascontiguousarray(np.asarray(a, np.float32))
    shared = {}
    for k_dev, k_in in (("w_in", "w_in"), ("w_out", "w_out"), ("wq", "mem_wq"), ("wk", "mem_wk"), ("wv", "mem_wv"),
                        ("wo", "mem_wo"), ("w_up", "ffn_w_up"), ("w_down", "ffn_w_down"), ("w_glu", "s5_w_glu"),
                        ("w1", "w1"), ("a1", "a1"), ("v1", "v1"), ("g1", "g1"), ("w2", "w2"), ("a2", "a2"),
                        ("v2", "v2"), ("g2", "g2")):
        shared[k_dev] = f(inp[k_in])
    shared["pvec"] = np.stack([pack_pvec(inp, l) for l in range(2)])
    BC = [s5_blocks(inp, l) for l in range(2)]
    shared["s5B"] = f(np.stack([x[0] for x in BC]).reshape(2, 2, 128, 1024))
    shared["s5C"] = f(np.stack([x[1] for x in BC]).reshape(2, 2, 128, 1024))
    shared["lruA"] = f(np.stack([lru_blocks(inp["lru_wa"][l]) for l in range(2)]).reshape(2, 128, 256))
    shared["lruI"] = f(np.stack([lru_blocks(inp["lru_wi"][l]) for l in range(2)]).reshape(2, 128, 256))
    maps = []
    for c in (range(NCORES) if cores is None else cores):
        sq = slice(c * NSS, (c + 1) * NSS)
        m = dict(shared)
        m["xp"] = f(np.asarray(inp["x_prompt"][c]).T)
        m["xs"] = f(np.asarray(inp["x_sample"][sq]).reshape(NSS * LS, D).T)
        m["mem"] = f(np.asarray(inp["mem_prompt"][c]).T)
        m["sshift"] = f(np.asarray(inp["state_shift"][:, sq]).transpose(0, 2, 1))
        w = np.asarray(inp["state_wkv"][:, sq]).reshape(2, NSS, 4, 2, 64, 64)
        m["swkv"] = f(w.transpose(0, 3, 5, 1, 2, 4).reshape(2, 128, NSS * 256))
        m["sconv"] = f(np.asarray(inp["state_conv"][:, sq]).transpose(0, 3, 1, 2).reshape(2, 256, NSS * 3))
        m["slru"] = f(np.asarray(inp["state_lru"][:, sq]).transpose(0, 2, 1))
        for nm, key in (("ss5re", "state_s5_re"), ("ss5im", "state_s5_im")):
            a = np.asarray(inp[key][:, sq]).reshape(2, NSS, 8, 2, 64)
            m[nm] = f(a.transpose(0, 3, 4, 2, 1).reshape(2, 128, 8 * NSS))
        k_ = np.asarray(inp["cache_mem_k"][:, sq]).reshape(2, NSS, NMEM, 4, 2, 128)
        m["ck"] = f(k_.transpose(0, 1, 5, 3, 4, 2).reshape(2, NSS, 128, 2048))
        v_ = np.asarray(inp["cache_mem_v"][:, sq]).reshape(2, NSS, 2, 128, D)
        m["cv"] = f(v_.transpose(0, 1, 3, 2, 4).reshape(2, NSS, 128, 2048))
        maps.append(m)
    return maps


def assemble(results):
    R = results
    cat = lambda fn: np.ascontiguousarray(np.concatenate([fn(r) for r in R], axis=0 if True else 0))
    y_prompt = np.stack([r["yp"].T for r in R])
    y_sample = np.concatenate([r["ys"].T.reshape(NSS, LS, D) for r in R], 0)
    vec = lambda a: a.transpose(0, 2, 1).reshape(2, -1)
    shift_p = np.stack([vec(r["shp"]) for r in R], 1)
    shift_s = np.concatenate([r["shs"].reshape(2, 128, 8, NSS).transpose(0, 3, 2, 1).reshape(2, NSS, D) for r in R], 1)
    wkv_p = np.stack([r["wkvp"].reshape(2, 2, 64, 4, 64).transpose(0, 3, 1, 4, 2).reshape(2, 8, 64, 64) for r in R], 1)
    wkv_s = np.concatenate([r["wkvs"].reshape(2, 2, 64, NSS, 4, 64).transpose(0, 3, 4, 1, 5, 2).reshape(2, NSS, 8, 64, 64)
                            for r in R], 1)
    conv_p = np.stack([r["convp"].reshape(2, 128, 2, 3).transpose(0, 3, 2, 1).reshape(2, 3, 256) for r in R], 1)
    conv_s = np.concatenate([r["convs"].reshape(2, 128, 2, NSS, 3).transpose(0, 3, 4, 2, 1).reshape(2, NSS, 3, 256) for r in R], 1)
    lru_p = np.stack([r["lrup"].reshape(2, 256) for r in R], 1)
    lru_s = np.concatenate([r["lrus"].reshape(2, 128, 2, NSS).transpose(0, 3, 2, 1).reshape(2, NSS, 256) for r in R], 1)

    def s5p_(k):
        return np.stack([r[k].reshape(2, 2, 64, 8).transpose(0, 3, 1, 2).reshape(2, 16, 64) for r in R], 1)

    def s5s_(k):
        return np.concatenate([r[k].reshape(2, 2, 64, 8, NSS).transpose(0, 4, 3, 1, 2).reshape(2, NSS, 16, 64) for r in R], 1)
    mem_k = np.stack([r["memk"].reshape(2, NMEM, 4, 256) for r in R], 1)
    mem_v = np.stack([r["memv"].reshape(2, NMEM, 4, 256) for r in R], 1)
    outs = (y_prompt, y_sample, shift_p, shift_s, wkv_p, wkv_s, conv_p, conv_s, lru_p, lru_s,
            s5p_("s5rep"), s5s_("s5res"), s5p_("s5imp"), s5s_("s5ims"), mem_k, mem_v)
    return tuple(np.ascontiguousarray(o, dtype=np.float32) for o in outs)


_NC_CACHE = {}


def kernel(**inputs):
    if "nc" not in _NC_CACHE:
        _NC_CACHE["nc"] = build()[0]
    nc = _NC_CACHE["nc"]
    maps = make_in_maps(inputs)
    res = run_bass_kernel_spmd(nc, maps, core_ids=list(range(NCORES)))
    return assemble(res.results)
```
